# Optimizing a Trainium2 kernel written in Bass

```python
import functools
import jax
import jax.numpy as jnp
from jax import lax
import numpy as np

D_MODEL = 2048
BATCH = 2
SEQ = 4096
DEPTH = 2

GRID_W = 64
CTX_LEN = 256
D_FF = -(-(8 * D_MODEL) // (3 * 256)) * 256

MIX_A = D_MODEL // 2
RWKV_HEAD = 64
RWKV_HEADS = MIX_A // RWKV_HEAD
LORA_W = 3 * D_MODEL // 64
LORA_A = 3 * D_MODEL // 64
LORA_G = D_MODEL // 8
A_COLS = 3 * MIX_A + LORA_W + LORA_A + LORA_G
RWKV_SPLITS = [MIX_A, 2 * MIX_A, 3 * MIX_A, 3 * MIX_A + LORA_W, 3 * MIX_A + LORA_W + LORA_A]
RWKV_GN_EPS = 64e-5

MIX_B = D_MODEL - MIX_A
GLA_HEADS = 4
GLA_KEY = MIX_B // 2
GLA_DK = GLA_KEY // GLA_HEADS
GLA_DV = MIX_B // GLA_HEADS
GLA_GATE_RANK = 16
GLA_GATE_TEMP = 16.0
B_COLS = 2 * GLA_KEY + 2 * MIX_B + GLA_GATE_RANK
GLA_SPLITS = [GLA_KEY, 2 * GLA_KEY, 2 * GLA_KEY + MIX_B, 2 * GLA_KEY + 2 * MIX_B]
AB_COLS = A_COLS + B_COLS

RET_HEADS = 8
RET_DK = D_MODEL // RET_HEADS
RET_DV = 2 * D_MODEL // RET_HEADS
RET_COLS = 6 * D_MODEL
RET_SPLITS = [D_MODEL, 2 * D_MODEL, 4 * D_MODEL]

CHUNK = 64
ROPE_THETA = 10000.0
RMS_EPS = 1e-6

kernel_name = "hybrid_rwkv7_gla_retention_dit"


def rmsnorm(x, g):
    xf = x.astype(jnp.float32)
    y = xf * lax.rsqrt(jnp.mean(xf * xf, axis=-1, keepdims=True) + RMS_EPS) * g.astype(jnp.float32)
    return y.astype(x.dtype)


def head_rmsnorm(t, g=None):
    y = t * lax.rsqrt(jnp.mean(t * t, axis=-1, keepdims=True) + RMS_EPS)
    return y if g is None else y * g


def modulate(h, shift, scale):
    return h * (1.0 + scale) + shift


def to_heads(t, n_heads):
    return t.reshape(*t.shape[:-1], n_heads, t.shape[-1] // n_heads)


def from_heads(t):
    return t.reshape(*t.shape[:-2], t.shape[-2] * t.shape[-1])


def centred_shift(z):
    zp = jnp.pad(z, ((0, 0), (1, 1), (0, 0)))
    return 0.5 * (zp[:, :-2] + zp[:, 2:])


def swiglu(h, w_in, w_out):
    gate, up = jnp.split(h @ w_in, 2, axis=-1)
    return (jax.nn.silu(gate) * up) @ w_out


def rotary_tables(rows, cols):
    nf = RET_DK // 4
    inv = ROPE_THETA ** (-jnp.arange(nf, dtype=jnp.float32) / nf)
    ang = jnp.concatenate([rows[:, None] * inv, cols[:, None] * inv], axis=-1)
    return jnp.cos(ang)[None, :, None, :], jnp.sin(ang)[None, :, None, :]


def apply_rotary(t, cos, sin):
    t1, t2 = jnp.split(t, 2, axis=-1)
    return jnp.concatenate([t1 * cos - t2 * sin, t1 * sin + t2 * cos], axis=-1)


def _chunks(t):
    b, T, h, e = t.shape
    return t.reshape(b, T // CHUNK, CHUNK, h, e).transpose(1, 0, 3, 2, 4)


def _unchunk(t):
    n, b, h, c, e = t.shape
    return t.transpose(1, 0, 3, 2, 4).reshape(b, n * c, h, e)


def ctx_then_latent(scan_fn, ctx_in, lat_in, s0, reverse):
    flip = (lambda t: jnp.flip(t, axis=1)) if reverse else (lambda t: t)
    y_c, s_c = scan_fn(*[flip(t) for t in ctx_in], s0)
    y_l, _ = scan_fn(*[flip(t) for t in lat_in], s_c)
    return flip(y_c), flip(y_l)


def rwkv7_scan(r, decay, k, v, a, b, s0):
    def step(s, inp):
        r_t, w_t, k_t, v_t, a_t, b_t = inp
        sa = jnp.einsum('bhvk,bhk->bhv', s, a_t)
        s = s * w_t[:, :, None, :] + sa[..., None] * b_t[:, :, None, :] + v_t[..., None] * k_t[:, :, None, :]
        return s, jnp.einsum('bhvk,bhk->bhv', s, r_t)
    xs = tuple(jnp.swapaxes(t, 0, 1) for t in (r, decay, k, v, a, b))
    s_fin, ys = lax.scan(step, s0, xs)
    return jnp.swapaxes(ys, 0, 1), s_fin


def rwkv7_prep(z, mu, w0, w2, a0, a2, g2, k_k, k_a):
    z = z.astype(jnp.float32)
    z = z + (centred_shift(z) - z) * mu
    r, k, v, wd, ad, gd = jnp.split(z, RWKV_SPLITS, axis=-1)
    lr = jax.nn.sigmoid(a0 + ad @ a2)
    kk = to_heads(k * k_k, RWKV_HEADS)
    kk = kk / jnp.maximum(jnp.sqrt(jnp.sum(kk * kk, axis=-1, keepdims=True)), 1e-12)
    k = k * (1.0 + (lr - 1.0) * k_a)
    g = jax.nn.sigmoid(gd) @ g2
    rh, kh, vh, lrh = (to_heads(t, RWKV_HEADS) for t in (r, k, v, lr))
    wt = jnp.tanh(wd)
    scan_in = []
    for d in range(2):
        w_log = -jax.nn.softplus(-(w0[d] + wt @ w2[d])) - 0.5
        decay = to_heads(jnp.exp(-jnp.exp(w_log)), RWKV_HEADS)
        scan_in.append((rh, decay, kh, vh, -kk, kk * lrh))
    return scan_in, (rh, kh, vh, g)


def rwkv7_finish(y, aux, r_k, ln_w, ln_b):
    rh, kh, vh, g = aux
    mean = jnp.mean(y, axis=-1, keepdims=True)
    var = jnp.mean(jnp.square(y - mean), axis=-1, keepdims=True)
    yn = from_heads((y - mean) * lax.rsqrt(var + RWKV_GN_EPS)) * ln_w + ln_b
    bonus = jnp.sum(rh * kh * r_k, axis=-1, keepdims=True) * vh
    return (yn + from_heads(bonus)) * g


def gla_chunk_scan(q, k, v, g, s0):
    q, k, v, g = (_chunks(t) for t in (q, k, v, g))
    bcum = jnp.cumsum(g, axis=3)
    b_last = bcum[..., -1:, :]
    q_in = q * jnp.exp(bcum)
    k_in = k * jnp.exp(-bcum)
    k_out = k * jnp.exp(b_last - bcum)
    mask = jnp.tril(jnp.ones((CHUNK, CHUNK), jnp.float32))
    att = jnp.einsum('nbhik,nbhjk->nbhij', q_in, k_in) * mask
    o_intra = jnp.einsum('nbhij,nbhjv->nbhiv', att, v)

    def step(s, inp):
        qi, ko, vv, dl = inp
        o = jnp.einsum('bhik,bhkv->bhiv', qi, s)
        s = s * dl[:, :, 0, :, None] + jnp.einsum('bhjk,bhjv->bhkv', ko, vv)
        return s, o
    s_fin, o_cross = lax.scan(step, s0, (q_in, k_out, v, jnp.exp(b_last)))
    return _unchunk(o_intra + o_cross), s_fin


def gla_prep(z, gk_up, gk_b):
    z = z.astype(jnp.float32)
    q, k, v, r, gd = jnp.split(z, GLA_SPLITS, axis=-1)
    qh = to_heads(q, GLA_HEADS) * GLA_DK ** -0.5
    kh = to_heads(k, GLA_HEADS)
    vh = to_heads(v, GLA_HEADS)
    scan_in = []
    for d in range(2):
        lg = jax.nn.log_sigmoid(gd @ gk_up[d] + gk_b[d]) / GLA_GATE_TEMP
        scan_in.append((qh, kh, vh, to_heads(lg, GLA_HEADS)))
    return scan_in, r


def rwkv_gla_mixer(h_lat, h_ctx, w_in, w_out, mu, w0, w2, a0, a2, g2, k_k, k_a, r_k,
                   ln_w, ln_b, gk_up, gk_b, gla_g, ctx_out):
    bsz = h_lat.shape[0]
    z_l = h_lat @ w_in
    z_c = h_ctx @ w_in
    ra_l, aux_l = rwkv7_prep(z_l[..., :A_COLS], mu, w0, w2, a0, a2, g2, k_k, k_a)
    ra_c, aux_c = rwkv7_prep(z_c[..., :A_COLS], mu, w0, w2, a0, a2, g2, k_k, k_a)
    gb_l, gate_l = gla_prep(z_l[..., A_COLS:], gk_up, gk_b)
    gb_c, gate_c = gla_prep(z_c[..., A_COLS:], gk_up, gk_b)
    s0_a = jnp.zeros((bsz, RWKV_HEADS, RWKV_HEAD, RWKV_HEAD), jnp.float32)
    s0_b = jnp.zeros((bsz, GLA_HEADS, GLA_DK, GLA_DV), jnp.float32)
    ya_c = ya_l = yb_c = yb_l = 0.0
    for d, rev in enumerate((False, True)):
        yc, yl = ctx_then_latent(rwkv7_scan, ra_c[d], ra_l[d], s0_a, rev)
        ya_c, ya_l = ya_c + yc, ya_l + yl
        yc, yl = ctx_then_latent(gla_chunk_scan, gb_c[d], gb_l[d], s0_b, rev)
        yb_c, yb_l = yb_c + yc, yb_l + yl

    def finish(ya, aux, yb, gate, h):
        out_a = rwkv7_finish(ya, aux, r_k, ln_w, ln_b)
        out_b = from_heads(head_rmsnorm(yb, gla_g)) * jax.nn.silu(gate)
        return jnp.concatenate([out_a, out_b], axis=-1).astype(h.dtype) @ w_out
    y_l = finish(ya_l, aux_l, yb_l, gate_l, h_lat)
    y_c = finish(ya_c, aux_c, yb_c, gate_c, h_ctx) if ctx_out else None
    return y_l, y_c


def retention_scan(gamma, q, k, v, s0):
    q, k, v = (_chunks(t) for t in (q, k, v))
    log_g = jnp.log(gamma)[:, None]
    pos = jnp.arange(CHUNK, dtype=jnp.float32)
    rel = pos[:, None] - pos[None, :]
    intra = jnp.where(rel[None] >= 0, jnp.exp(rel[None] * log_g[..., None]), 0.0)
    q_dec = q * jnp.exp((pos + 1.0) * log_g)[..., None]
    k_dec = k * jnp.exp((CHUNK - 1.0 - pos) * log_g)[..., None]
    chunk_dec = jnp.exp(CHUNK * log_g)[..., None]
    att = jnp.einsum('nbhik,nbhjk->nbhij', q, k) * intra
    o_intra = jnp.einsum('nbhij,nbhjv->nbhiv', att, v)

    def step(s, inp):
        qd, kd, vv = inp
        o = jnp.einsum('bhik,bhkv->bhiv', qd, s)
        s = s * chunk_dec + jnp.einsum('bhjk,bhjv->bhkv', kd, vv)
        return s, o
    s_fin, o_cross = lax.scan(step, s0, (q_dec, k_dec, v))
    return _unchunk(o_intra + o_cross), s_fin


def retention_mixer(h_lat, h_ctx, w_in, w_out, cos, sin, ctx_out):
    bsz = h_lat.shape[0]
    gamma = 1.0 - 2.0 ** (-5.0 - jnp.arange(RET_HEADS, dtype=jnp.float32))
    scan = functools.partial(retention_scan, gamma)

    def prep(h, rotate):
        z = (h @ w_in).astype(jnp.float32)
        q, k, v, g = jnp.split(z, RET_SPLITS, axis=-1)
        q, k = to_heads(q, RET_HEADS), to_heads(k, RET_HEADS)
        if rotate:
            q, k = apply_rotary(q, cos, sin), apply_rotary(k, cos, sin)
        return (q, k * RET_DK ** -0.5, to_heads(v, RET_HEADS)), g
    in_l, g_l = prep(h_lat, True)
    in_c, g_c = prep(h_ctx, False)
    s0 = jnp.zeros((bsz, RET_HEADS, RET_DK, RET_DV), jnp.float32)
    o_c = o_l = 0.0
    for rev in (False, True):
        yc, yl = ctx_then_latent(scan, in_c, in_l, s0, rev)
        o_c, o_l = o_c + yc, o_l + yl

    def finish(o, g, h):
        return (from_heads(head_rmsnorm(o)) * jax.nn.silu(g)).astype(h.dtype) @ w_out
    y_l = finish(o_l, g_l, h_lat)
    y_c = finish(o_c, g_c, h_ctx) if ctx_out else None
    return y_l, y_c


def mixer_residual(h, y, mod, ng):
    return h + mod[2] * rmsnorm(y, ng[1])


def ffn_residual(h, mod, ng, w_in, w_out):
    f = swiglu(modulate(rmsnorm(h, ng[2]), mod[3], mod[4]), w_in, w_out)
    return h + mod[5] * rmsnorm(f, ng[3])


def setup_inputs(seed: int = 0) -> dict:
    key = jax.random.key(seed)
    ks = iter(jax.random.split(key, 32))
    D = D_MODEL
    NE = (DEPTH + 1) // 2
    NO = DEPTH // 2

    def nrm(shape, scale):
        return jax.random.normal(next(ks), shape, jnp.float32) * scale
    return {
        "x": nrm((BATCH, SEQ, D), 1.0),
        "c": nrm((BATCH, D), 1.0),
        "ctx": nrm((BATCH, CTX_LEN, D), 1.0),
        "c_ctx": nrm((D,), 1.0),
        "mod_w": nrm((DEPTH, D, 6 * D), 0.5 * D ** -0.5),
        "mod_b": nrm((DEPTH, 6 * D), 0.02),
        "norm_g": 1.0 + nrm((DEPTH, 4, D), 0.05),
        "ffn_w_in": nrm((DEPTH, D, 2 * D_FF), D ** -0.5),
        "ffn_w_out": nrm((DEPTH, D_FF, D), D_FF ** -0.5),
        "ab_w_in": nrm((NE, D, AB_COLS), D ** -0.5),
        "ab_w_out": nrm((NE, MIX_A + MIX_B, D), (MIX_A + MIX_B) ** -0.5),
        "rwkv_mu": jax.random.uniform(next(ks), (NE, A_COLS), jnp.float32),
        "rwkv_w0": nrm((NE, 2, MIX_A), 1.0),
        "rwkv_w2": nrm((NE, 2, LORA_W, MIX_A), LORA_W ** -0.5),
        "rwkv_a0": nrm((NE, MIX_A), 0.5),
        "rwkv_a2": nrm((NE, LORA_A, MIX_A), LORA_A ** -0.5),
        "rwkv_g2": nrm((NE, LORA_G, MIX_A), LORA_G ** -0.5),
        "rwkv_kk": 0.85 + nrm((NE, MIX_A), 0.05),
        "rwkv_ka": 1.0 + nrm((NE, MIX_A), 0.05),
        "rwkv_rk": nrm((NE, RWKV_HEADS, RWKV_HEAD), 0.1),
        "rwkv_ln_w": 1.0 + nrm((NE, MIX_A), 0.05),
        "rwkv_ln_b": nrm((NE, MIX_A), 0.02),
        "gla_gk_up": nrm((NE, 2, GLA_GATE_RANK, GLA_KEY), GLA_GATE_RANK ** -0.5),
        "gla_gk_b": nrm((NE, 2, GLA_KEY), 0.5),
        "gla_norm_g": 1.0 + nrm((NE, GLA_DV), 0.05),
        "ret_w_in": nrm((NO, D, RET_COLS), D ** -0.5),
        "ret_w_out": nrm((NO, 2 * D, D), (2 * D) ** -0.5),
    }


def reference(x, c, ctx, c_ctx, mod_w, mod_b, norm_g, ffn_w_in, ffn_w_out, ab_w_in, ab_w_out,
              rwkv_mu, rwkv_w0, rwkv_w2, rwkv_a0, rwkv_a2, rwkv_g2, rwkv_kk, rwkv_ka, rwkv_rk,
              rwkv_ln_w, rwkv_ln_b, gla_gk_up, gla_gk_b, gla_norm_g, ret_w_in, ret_w_out):
    bsz, n_lat = x.shape[0], x.shape[1]
    rows_n = n_lat // GRID_W
    rows = jnp.repeat(jnp.arange(rows_n, dtype=jnp.float32), GRID_W)
    cols = jnp.tile(jnp.arange(GRID_W, dtype=jnp.float32), rows_n)
    cos, sin = rotary_tables(rows, cols)
    silu_c = jax.nn.silu(c)
    silu_cc = jax.nn.silu(c_ctx)
    h_lat, h_ctx = x, ctx
    for i in range(DEPTH):
        last = i == DEPTH - 1
        j = i // 2
        mod_l = (silu_c @ mod_w[i] + mod_b[i]).reshape(bsz, 6, D_MODEL).transpose(1, 0, 2)[:, :, None, :]
        mod_c = (silu_cc @ mod_w[i] + mod_b[i]).reshape(6, D_MODEL)
        ng = norm_g[i]
        a_l = modulate(rmsnorm(h_lat, ng[0]), mod_l[0], mod_l[1])
        a_c = modulate(rmsnorm(h_ctx, ng[0]), mod_c[0], mod_c[1])
        if i % 2 == 0:
            y_l, y_c = rwkv_gla_mixer(a_l, a_c, ab_w_in[j], ab_w_out[j], rwkv_mu[j], rwkv_w0[j],
                                      rwkv_w2[j], rwkv_a0[j], rwkv_a2[j], rwkv_g2[j], rwkv_kk[j],
                                      rwkv_ka[j], rwkv_rk[j], rwkv_ln_w[j], rwkv_ln_b[j],
                                      gla_gk_up[j], gla_gk_b[j], gla_norm_g[j], not last)
        else:
            y_l, y_c = retention_mixer(a_l, a_c, ret_w_in[j], ret_w_out[j], cos, sin, not last)
        h_lat = mixer_residual(h_lat, y_l, mod_l, ng)
        h_lat = ffn_residual(h_lat, mod_l, ng, ffn_w_in[i], ffn_w_out[i])
        if not last:
            h_ctx = mixer_residual(h_ctx, y_c, mod_c, ng)
            h_ctx = ffn_residual(h_ctx, mod_c, ng, ffn_w_in[i], ffn_w_out[i])
    return h_lat
```

```python
import numpy as np
import ml_dtypes
from contextlib import ExitStack
import concourse.bass as bass
import concourse.mybir as mybir
from concourse.bass_utils import run_bass_kernel_spmd


F32 = mybir.dt.float32
BF16 = mybir.dt.bfloat16
AF = mybir.ActivationFunctionType
ALU = mybir.AluOpType
AX = mybir.AxisListType

ENGS = ("pe", "act", "dve", "pool", "sp")
N_DMA_SEMS = 40


class Tl:
    __slots__ = ("ap", "name", "lw", "rd", "pw")

    def __init__(self, ap, name=""):
        self.ap = ap
        self.name = name
        self.lw = None
        self.rd = []
        self.pw = []

    def __getitem__(self, idx):
        return Vw(self, self.ap[idx])

    def v(self):
        return Vw(self, self.ap)

    def sub(self, idx, name=""):
        return Tl(self.ap[idx], name or self.name)


class Vw:
    __slots__ = ("t", "ap")

    def __init__(self, t, ap):
        self.t = t
        self.ap = ap

    def __getitem__(self, idx):
        return Vw(self.t, self.ap[idx])

    def bitcast(self, dt):
        return Vw(self.t, self.ap.bitcast(dt))

    def rearrange(self, *a, **k):
        return Vw(self.t, self.ap.rearrange(*a, **k))

    def with_ap(self, ap):
        return Vw(self.t, ap)


class Op:
    __slots__ = ("eng", "fn", "deps", "is_dma", "tok", "sig", "idx")


class Prog:
    def __init__(self, nc, arena_words=51200, same_engine_sync=True):
        self.nc = nc
        self.ops = []
        self.same_engine_sync = same_engine_sync
        self.arena_words = arena_words
        self.arena = None
        self.off = 0
        self.scopes = []
        self.dma_since_barrier = []
        self.last_on_eng = {e: None for e in ENGS}
        self.psum = []
        self.ndma = 0
        self.high = 0
        self.dma_ops = []

    def setup_mem(self, stack):
        nc = self.nc
        self.arena = stack.enter_context(nc.sbuf_tensor("arena", [128, self.arena_words], F32))
        for i in range(8):
            t = stack.enter_context(nc.psum_tensor(f"psb{i}", [128, 512], F32))
            self.psum.append(Tl(t[:, :], f"ps{i}"))
        self.ps_rr = 0

    def ps(self):
        t = self.psum[self.ps_rr % 8]
        self.ps_rr += 1
        return t

    def alloc(self, shape, dtype=F32, name=""):
        p = shape[0]
        free = int(np.prod(shape[1:]))
        isz = 2 if dtype == BF16 else 4
        nw = (free * isz + 3) // 4
        nw = (nw + 7) // 8 * 8
        if self.off + nw > self.arena_words:
            raise RuntimeError(f"arena overflow allocating {name} {shape}: off={self.off} nw={nw}")
        ap = self.arena[0:p, self.off:self.off + nw]
        self.off += nw
        self.high = max(self.high, self.off)
        if dtype != F32:
            ap = ap.bitcast(dtype)
        ap = ap[:, 0:free]
        if len(shape) == 3:
            ap = ap.rearrange("p (a b) -> p a b", a=shape[1])
        elif len(shape) == 4:
            ap = ap.rearrange("p (a b c) -> p a b c", a=shape[1], b=shape[2])
        return Tl(ap, name)

    def alloc_at(self, byte_off, shape, dtype=F32, name=""):
        p = shape[0]
        free = int(np.prod(shape[1:]))
        isz = 2 if dtype == BF16 else 4
        nw = (free * isz + 3) // 4
        assert byte_off % 32 == 0
        w0 = byte_off // 4
        if w0 + nw > self.arena_words:
            raise RuntimeError(f"arena overflow allocating {name} {shape} at {byte_off}")
        ap = self.arena[0:p, w0:w0 + nw]
        if dtype != F32:
            ap = ap.bitcast(dtype)
        ap = ap[:, 0:free]
        if len(shape) == 3:
            ap = ap.rearrange("p (a b) -> p a b", a=shape[1])
        elif len(shape) == 4:
            ap = ap.rearrange("p (a b c) -> p a b c", a=shape[1], b=shape[2])
        return Tl(ap, name)

    def push(self):
        self.scopes.append(self.off)

    def pop(self):
        self.barrier()
        self.off = self.scopes.pop()

    def _add(self, eng, fn, reads, writes, is_dma=False, pwrites=()):
        op = Op()
        op.eng = eng
        op.fn = fn
        op.is_dma = is_dma
        op.idx = len(self.ops)
        op.sig = False
        op.tok = None
        deps = set()
        for v in reads:
            t = v.t if isinstance(v, Vw) else v
            if t.lw is not None:
                deps.add(t.lw)
            deps.update(t.pw)
        for v in writes:
            t = v.t if isinstance(v, Vw) else v
            if t.lw is not None:
                deps.add(t.lw)
            deps.update(t.rd)
            deps.update(t.pw)
        for v in pwrites:
            t = v.t if isinstance(v, Vw) else v
            if t.lw is not None:
                deps.add(t.lw)
            deps.update(t.rd)
        if is_dma:
            k = self.ndma
            self.ndma += 1
            op.tok = ("dma", k % N_DMA_SEMS, 16 * (k // N_DMA_SEMS + 1))
            if k >= N_DMA_SEMS:
                deps.add(self.dma_ops[k - N_DMA_SEMS])
            self.dma_ops.append(op.idx)
            self.dma_since_barrier.append(op.idx)
        op.deps = deps
        self.ops.append(op)
        for v in reads:
            t = v.t if isinstance(v, Vw) else v
            t.rd.append(op.idx)
        for v in writes:
            t = v.t if isinstance(v, Vw) else v
            t.lw = op.idx
            t.rd = []
            t.pw = []
        for v in pwrites:
            t = v.t if isinstance(v, Vw) else v
            t.pw.append(op.idx)
        self.last_on_eng[eng] = op.idx
        return op

    def barrier(self):
        lasts = [i for i in self.last_on_eng.values() if i is not None]
        lasts += self.dma_since_barrier
        self.dma_since_barrier = []
        for e in ENGS:
            op = self._add(e, lambda eh: eh.nop(), [], [])
            op.deps.update(lasts)

    @staticmethod
    def _ap(x):
        return x.ap if isinstance(x, Vw) else x

    def matmul(self, out, lhsT, rhs, start=True, stop=True, **kw):
        o, l, r = out.ap, lhsT.ap, rhs.ap
        return self._add("pe", lambda e: e.matmul(o, l, r, start=start, stop=stop, **kw),
                         [lhsT, rhs], [out])

    def transpose(self, out, in_, ident):
        o, i, d = out.ap, in_.ap, ident.ap
        return self._add("pe", lambda e: e.transpose(o, i, d), [in_, ident], [out])

    def act(self, out, in_, func, bias=None, scale=None, accum_out=None, eng="act"):
        kw = {}
        reads = [in_]
        writes = [out]
        if bias is not None:
            if isinstance(bias, Vw):
                reads.append(bias)
                kw["bias"] = bias.ap
            else:
                kw["bias"] = bias
        if scale is not None:
            if isinstance(scale, Vw):
                reads.append(scale)
                kw["scale"] = scale.ap
            else:
                kw["scale"] = scale
        if accum_out is not None:
            writes.append(accum_out)
            kw["accum_out"] = accum_out.ap
        o, i = out.ap, in_.ap
        return self._add(eng, lambda e: e.activation(o, i, func, **kw), reads, writes)

    def tt(self, out, in0, in1, op, eng="dve"):
        o, a, b = out.ap, in0.ap, in1.ap
        return self._add(eng, lambda e: e.tensor_tensor(o, a, b, op), [in0, in1], [out])

    def ts(self, out, in0, s1, op0, s2=None, op1=None, eng="dve", accum_out=None):
        reads = [in0]
        writes = [out]
        a1 = s1
        a2 = s2
        if isinstance(s1, Vw):
            reads.append(s1)
            a1 = s1.ap
        if isinstance(s2, Vw):
            reads.append(s2)
            a2 = s2.ap
        o, i = out.ap, in0.ap
        kw = {}
        if accum_out is not None:
            writes.append(accum_out)
            kw["accum_out"] = accum_out.ap
        if op1 is None:
            return self._add(eng, lambda e: e.tensor_scalar(o, i, a1, None, op0, **kw), reads, writes)
        return self._add(eng, lambda e: e.tensor_scalar(o, i, a1, a2, op0, op1, **kw), reads, writes)

    def stt(self, out, in0, scalar, in1, op0, op1, eng="dve"):
        reads = [in0, in1]
        s = scalar
        if isinstance(scalar, Vw):
            reads.append(scalar)
            s = scalar.ap
        o, a, b = out.ap, in0.ap, in1.ap
        return self._add(eng, lambda e: e.scalar_tensor_tensor(o, a, s, b, op0, op1), reads, [out])

    def copy(self, out, in_, eng="dve"):
        o, i = out.ap, in_.ap
        if eng == "act":
            return self._add(eng, lambda e: e.copy(o, i), [in_], [out])
        return self._add(eng, lambda e: e.tensor_copy(o, i), [in_], [out])

    def memset(self, out, val, eng="dve"):
        o = out.ap
        return self._add(eng, lambda e: e.memset(o, val), [], [out])

    def scan(self, out, d0, d1, initial, op0, op1):
        reads = [d0, d1]
        ini = initial
        if isinstance(initial, Vw):
            reads.append(initial)
            ini = initial.ap
        o, a, b = out.ap, d0.ap, d1.ap
        return self._add("dve", lambda e: e.tensor_tensor_scan(o, a, b, ini, op0, op1), reads, [out])

    def recip(self, out, in_):
        o, i = out.ap, in_.ap
        return self._add("dve", lambda e: e.reciprocal(o, i), [in_], [out])

    def dma(self, out, in_, q="sp", reads=None, writes=None, pwrites=()):
        o, i = self._ap(out), self._ap(in_)
        rd = [in_] if isinstance(in_, Vw) else []
        wr = [out] if isinstance(out, Vw) else []
        if reads:
            rd += reads
        if writes:
            wr += writes
        return self._add(q, lambda e: e.dma_start(out=o, in_=i), rd, wr, is_dma=True, pwrites=pwrites)

    def raw(self, eng, fn, reads, writes):
        return self._add(eng, fn, reads, writes)

    def emit(self, stack, final_wait_ops=()):
        nc = self.nc
        ops = self.ops
        for op in ops:
            for d in op.deps:
                dop = ops[d]
                if dop.eng == "pe" and op.eng == "pe" and not dop.is_dma:
                    continue
                if (not self.same_engine_sync) and dop.eng == op.eng and not dop.is_dma:
                    continue
                dop.sig = True
        for i in final_wait_ops:
            ops[i].sig = True
        esem = {e: stack.enter_context(nc.semaphore(f"s_{e}")) for e in ENGS}
        dsem = [stack.enter_context(nc.semaphore(f"s_dma{i}")) for i in range(N_DMA_SEMS)]
        cnt = {e: 0 for e in ENGS}
        for op in ops:
            if op.is_dma:
                op.tok = (dsem[op.tok[1]], op.tok[2])
            elif op.sig:
                cnt[op.eng] += 1
                op.tok = (esem[op.eng], cnt[op.eng])
        self.sig_counts = dict(cnt)
        per = {e: [] for e in ENGS}
        for op in ops:
            per[op.eng].append(op)
        block = stack.enter_context(nc.Block())
        nwaits = {e: 0 for e in ENGS}

        def body(ename, extra_final):
            def run(eh):
                waited = {}
                for op in per[ename]:
                    need = {}
                    for d in op.deps:
                        dop = ops[d]
                        if not dop.is_dma:
                            if dop.eng == "pe" and ename == "pe":
                                continue
                            if (not self.same_engine_sync) and dop.eng == ename:
                                continue
                        sem, val = dop.tok
                        key = id(sem)
                        if waited.get(key, 0) >= val:
                            continue
                        if key not in need or need[key][1] < val:
                            need[key] = (sem, val)
                    for key, (sem, val) in need.items():
                        eh.wait_ge(sem, val)
                        waited[key] = val
                        nwaits[ename] += 1
                    ins = op.fn(eh)
                    if op.is_dma:
                        ins.then_inc(op.tok[0], 16)
                    elif op.sig:
                        ins.then_inc(op.tok[0], 1)
                if extra_final:
                    for i in final_wait_ops:
                        sem, val = ops[i].tok
                        eh.wait_ge(sem, val)
            return run

        block.tensor(body("pe", False))
        block.scalar(body("act", False))
        block.vector(body("dve", False))
        block.gpsimd(body("pool", False))
        block.sync(body("sp", True))
        self.nwaits = nwaits


D = 2048
KC_D = 16
DFF = 5632
RMS_EPS = 1e-6


def bc_mid(v, reps):
    ap = v.ap
    (ps, pn), (s1, n1) = ap.ap
    return Vw(v.t, bass.AP(ap.tensor, ap.offset, [[ps, pn], [0, reps], [s1, n1]]))


def bc_inner(v, reps):
    ap = v.ap
    (ps, pn), (s1, n1) = ap.ap
    return Vw(v.t, bass.AP(ap.tensor, ap.offset, [[ps, pn], [s1, n1], [0, reps]]))


def make_consts(P):
    c = {}
    ones = P.alloc([128, 128], BF16, "ones_bf")
    P.memset(ones.v(), 1.0)
    c["ones_bf"] = ones
    idf = P.alloc([128, 128], F32, "ident_f")
    P.memset(idf.v(), 1.0, eng="pool")
    ia = idf.ap
    P.raw("pool", lambda e: e.affine_select(ia, ia, [[-1, 128]], ALU.is_equal, 0.0, base=0,
                                            channel_multiplier=1), [idf.v()], [idf.v()])
    c["ident_f"] = idf
    idb = P.alloc([128, 128], BF16, "ident_b")
    P.copy(idb.v(), idf.v())
    c["ident_b"] = idb
    return c


def norm_stats(P, C, X, KC, n, sq, rstd, dim, eps=RMS_EPS, rs_eng="dve"):
    P.act(sq[:, 0:KC, 0:n], X, AF.Square)
    ps = P.ps()
    for c in range(KC):
        P.matmul(ps[:, 0:n], C["ones_bf"].v(), sq[:, c, 0:n], start=(c == 0), stop=(c == KC - 1))
    P.act(rstd[:, 0:n], ps[:, 0:n], AF.Sqrt, scale=1.0 / dim, bias=float(eps))
    P.recip(rstd[:, 0:n], rstd[:, 0:n])


def proj(P, Wt, groups, KC, rhs_tiles, epilogue, wbufs, q="pool", pair=None):
    ncg = Wt.shape[3]
    for gi, g in enumerate(groups):
        wb = wbufs[gi % len(wbufs)]
        P.dma(wb[:, 0:KC, 0:ncg], Wt[g], q=q)
        for mi in range((ncg + 127) // 128):
            msz = min(128, ncg - mi * 128)
            for ti, (rhs, n) in enumerate(rhs_tiles):
                ps = P.ps()
                for k in range(KC):
                    P.matmul(ps[0:msz, 0:n], wb[:, k, mi * 128:mi * 128 + msz], rhs[:, k, 0:n],
                             start=(k == 0), stop=(k == KC - 1))
                epilogue(gi, mi, ti, ps, msz, n)


def build_mods():
    nc = bass.Bass("TRN2", target_bir_lowering=False)
    cT = nc.dram_tensor("cT", [128, 16, 3], F32, kind="ExternalInput").ap()
    mw = nc.dram_tensor("mw", [2, 128, 16, 1536], F32, kind="ExternalInput").ap()
    mb = nc.dram_tensor("mb", [128, 2, 12], F32, kind="ExternalInput").ap()
    mo = nc.dram_tensor("mo", [128, 2, 12, 3], F32, kind="ExternalOutput").ap()
    P = Prog(nc)
    with ExitStack() as st:
        P.setup_mem(st)
        outs = emit_mods(P, cT, mw, mb, mo)
        P.emit(st, final_wait_ops=outs)
    return nc


def emit_mods(P, cT, mw, mb, mo):
    ct = P.alloc([128, 16, 3], F32, "ct")
    sc = P.alloc([128, 16, 3], F32, "sc")
    mbt = P.alloc([128, 2, 12], F32, "mbt")
    res = P.alloc([128, 2, 12, 3], F32, "res")
    w = P.alloc([128, 16, 1536], F32, "mw")
    P.dma(ct.v(), cT)
    P.dma(mbt.v(), mb)
    P.act(sc.v(), ct.v(), AF.Silu)
    for l in range(2):
        for h in range(2):
            P.dma(w[:, h * 8:(h + 1) * 8, :], mw[l, :, h * 8:(h + 1) * 8, :])
        for m in range(12):
            ps = P.ps()
            for k in range(16):
                P.matmul(ps[:, 0:3], w[:, k, m * 128:(m + 1) * 128], sc[:, k, :], start=(k == 0), stop=(k == 15))
            P.ts(res[:, l, m, :], ps[:, 0:3], mbt[:, l, m:m + 1], ALU.add)
    return [P.dma(mo, res.v()).idx]


def build_post(KCm, tiles, wout_g):
    NT = sum(n for n, _ in tiles)
    nc = bass.Bass("TRN2", target_bir_lowering=False)
    mT = nc.dram_tensor("mT", [KCm * 128, NT], BF16, kind="ExternalInput").ap()
    xT = nc.dram_tensor("xT", [D, NT], F32, kind="ExternalInput").ap()
    Gwo = D // wout_g
    wo = nc.dram_tensor("wo", [Gwo, 128, KCm, wout_g], F32, kind="ExternalInput").ap()
    wi = nc.dram_tensor("wi", [22, 128, 16, 512], F32, kind="ExternalInput").ap()
    w2 = nc.dram_tensor("w2", [16, 128, 44, 128], F32, kind="ExternalInput").ap()
    pv = nc.dram_tensor("pv", [128, 2, 7, 16], F32, kind="ExternalInput").ap()
    oT = nc.dram_tensor("oT", [D, NT], F32, kind="ExternalOutput").ap()
    hsp = Tl(nc.dram_tensor("hsp", [D, NT], F32, kind="Internal").ap(), "hsp")
    P = Prog(nc)
    with ExitStack() as st:
        P.setup_mem(st)
        emit_post(P, KCm, tiles, wout_g, mT, xT, wo, wi, w2, pv, oT, hsp, st)
    return nc


def emit_post(P, KCm, tiles, wout_g, mT, xT, wo, wi, w2, pv, oT, hsp, st, final=True):
    NT = sum(n for n, _ in tiles)
    Gwo = D // wout_g
    KB = 1024
    C = make_consts(P)
    pvt = P.alloc([128, 2, 7, 16], F32, "pv")
    P.dma(pvt.v(), pv)
    der = P.alloc([128, 2, 3, 16], F32, "der")
    for kd in range(2):
        P.tt(der[:, kd, 0, :], pvt[:, kd, 0, :], pvt[:, kd, 4, :], ALU.mult)
        P.ts(der[:, kd, 1, :], pvt[:, kd, 2, :], 1.0, ALU.add)
        P.tt(der[:, kd, 1, :], der[:, kd, 1, :], pvt[:, kd, 5, :], ALU.mult)
        P.tt(der[:, kd, 2, :], pvt[:, kd, 3, :], pvt[:, kd, 6, :], ALU.mult)
    sq = P.alloc([128, 16, 256], BF16, "sq")
    rstd = P.alloc([128, 256], F32, "rstd")
    assert P.off * 4 <= 12 * KB, P.off * 4
    subt = []
    ntl = []
    t0 = 0
    for n, kd in tiles:
        o = 0
        while o < n:
            m = min(256, n - o)
            subt.append((t0 + o, m, kd))
            o += m
        ntl.append((t0, n))
        t0 += n
    A2 = 12 * KB
    RB = 47 * KB
    RA = 117 * KB
    a2 = P.alloc_at(A2, [128, 16, NT], BF16, "a2")
    yT = P.alloc_at(RB, [128, 16, NT], F32, "yT")
    mt = P.alloc_at(RA, [128, KCm, NT], BF16, "mt")
    wsz = KCm * wout_g * 2
    assert KCm * NT * 2 + 2 * wsz <= 83 * KB
    wb = [P.alloc_at(RA + KCm * NT * 2 + i * wsz, [128, KCm, wout_g], BF16, f"wo{i}") for i in range(2)]
    P.dma(mt.v(), mT.rearrange("(c p) n -> p c n", p=128))
    flip = [0]

    def ep1(gi, mi, ti, ps, msz, n):
        ch = gi * (wout_g // 128) + mi
        a, nn = ntl[ti]
        flip[0] ^= 1
        P.copy(yT[:, ch, a:a + nn], ps[:, 0:nn], eng="act" if flip[0] else "dve")
    proj(P, wo, list(range(Gwo)), KCm, [(mt[:, :, a:a + n], n) for a, n in ntl], ep1, wb)
    P.barrier()
    xt = [P.alloc_at(RA + i * 16 * KB, [128, 16, 256], F32, f"xt{i}") for i in range(2)]
    tmp = P.alloc_at(RA + 32 * KB, [128, 16, 256], F32, "tmp")
    for si, (a, n, kd) in enumerate(subt):
        x_ = xt[si % 2]
        norm_stats(P, C, yT[:, :, a:a + n], 16, n, sq, rstd, D)
        P.dma(x_[:, :, 0:n], xT[:, a:a + n].rearrange("(c p) n -> p c n", p=128))
        P.tt(yT[:, :, a:a + n], yT[:, :, a:a + n], bc_mid(rstd[:, 0:n], 16), ALU.mult)
        for c in range(16):
            P.stt(yT[:, c, a:a + n], yT[:, c, a:a + n], der[:, kd, 0, c:c + 1], x_[:, c, 0:n], ALU.mult, ALU.add)
        P.dma(hsp[:, a:a + n].rearrange("(c p) n -> p c n", p=128), yT[:, :, a:a + n])
        norm_stats(P, C, yT[:, :, a:a + n], 16, n, sq, rstd, D)
        P.tt(tmp[:, :, 0:n], yT[:, :, a:a + n], bc_mid(rstd[:, 0:n], 16), ALU.mult)
        for c in range(16):
            P.act(a2[:, c, a:a + n], tmp[:, c, 0:n], AF.Identity, scale=der[:, kd, 1, c:c + 1], bias=pvt[:, kd, 1, c:c + 1])
    P.barrier()
    hm = P.alloc_at(RB, [128, 44, NT], BF16, "hm")
    WB0 = RB + 96 * KB
    wib = [P.alloc_at(WB0 + i * 16 * KB, [128, 16, 512], BF16, f"wi{i}") for i in range(2)]
    sgt = [P.alloc_at(WB0 + 32 * KB + i * 2 * KB, [128, 512], F32, f"sg{i}") for i in range(4)]
    sgi = [0]
    for g in range(22):
        wbuf = wib[g % 2]
        P.dma(wbuf.v(), wi[g], q="pool")
        for mi in range(2):
            ch = g * 2 + mi
            for (a, n) in ntl:
                psg = P.ps()
                for k in range(16):
                    P.matmul(psg[:, 0:n], wbuf[:, k, mi * 128:(mi + 1) * 128], a2[:, k, a:a + n], start=(k == 0), stop=(k == 15))
                psu = P.ps()
                for k in range(16):
                    P.matmul(psu[:, 0:n], wbuf[:, k, 256 + mi * 128:256 + (mi + 1) * 128], a2[:, k, a:a + n], start=(k == 0), stop=(k == 15))
                s_ = sgt[sgi[0] % 4]
                sgi[0] += 1
                P.act(s_[:, 0:n], psg[:, 0:n], AF.Silu)
                P.tt(hm[:, ch, a:a + n], s_[:, 0:n], psu[:, 0:n], ALU.mult)
    P.barrier()
    fa = P.alloc_at(A2, [128, 8, NT], F32, "fa")
    fb = P.alloc_at(WB0, [128, 8, NT], F32, "fb")
    w2b = [P.alloc_at(178 * KB + i * 11 * KB, [128, 44, 128], BF16, f"w2{i}") for i in range(2)]

    def ep3(gi, mi, ti, ps, msz, n):
        a, nn = ntl[ti]
        flip[0] ^= 1
        dst = fa if gi < 8 else fb
        P.copy(dst[:, gi % 8, a:a + nn], ps[:, 0:nn], eng="act" if flip[0] else "dve")
    proj(P, w2, list(range(16)), 44, [(hm[:, :, a:a + n], n) for a, n in ntl], ep3, w2b)
    P.barrier()
    h1t = [P.alloc_at(RB + i * 16 * KB, [128, 16, 256], F32, f"h1t{i}") for i in range(2)]
    outs = []
    for si, (a, n, kd) in enumerate(subt):
        h_ = h1t[si % 2]
        P.dma(h_[:, :, 0:n], hsp[:, a:a + n].rearrange("(c p) n -> p c n", p=128))
        P.act(sq[:, 0:8, 0:n], fa[:, :, a:a + n], AF.Square)
        P.act(sq[:, 8:16, 0:n], fb[:, :, a:a + n], AF.Square)
        ps = P.ps()
        for c in range(16):
            P.matmul(ps[:, 0:n], C["ones_bf"].v(), sq[:, c, 0:n], start=(c == 0), stop=(c == 15))
        P.act(rstd[:, 0:n], ps[:, 0:n], AF.Sqrt, scale=1.0 / D, bias=float(RMS_EPS))
        P.recip(rstd[:, 0:n], rstd[:, 0:n])
        P.tt(fa[:, :, a:a + n], fa[:, :, a:a + n], bc_mid(rstd[:, 0:n], 8), ALU.mult)
        P.tt(fb[:, :, a:a + n], fb[:, :, a:a + n], bc_mid(rstd[:, 0:n], 8), ALU.mult)
        for c in range(16):
            src = fa if c < 8 else fb
            P.stt(h_[:, c, 0:n], src[:, c % 8, a:a + n], der[:, kd, 2, c:c + 1], h_[:, c, 0:n], ALU.mult, ALU.add)
        outs.append(P.dma(oT[:, a:a + n].rearrange("(c p) n -> p c n", p=128), h_[:, :, 0:n]).idx)
    if final:
        P.emit(st, final_wait_ops=outs)
    return outs


RC = 128
T_CTX = 256
T_LAT = 4096
T_ALL = T_CTX + T_LAT


def build_ret():
    nc = bass.Bass("TRN2", target_bir_lowering=False)
    hT = nc.dram_tensor("hT", [D, T_ALL], F32, kind="ExternalInput").ap()
    wqkg = nc.dram_tensor("wqkg", [8, 128, 16, 256], F32, kind="ExternalInput").ap()
    wv = nc.dram_tensor("wv", [2, 128, 16, 512], F32, kind="ExternalInput").ap()
    pv = nc.dram_tensor("pv", [128, 2, 3, 16], F32, kind="ExternalInput").ap()
    cs = nc.dram_tensor("cs", [128, 2, T_LAT], F32, kind="ExternalInput").ap()
    mk = nc.dram_tensor("mk", [128, 2, 2, 128], F32, kind="ExternalInput").ap()
    dr = nc.dram_tensor("dr", [128, 2, 2, 128], F32, kind="ExternalInput").ap()
    kdc = nc.dram_tensor("kdc", [128, 2, 2], F32, kind="ExternalInput").ap()
    gC = nc.dram_tensor("gC", [128, 2], F32, kind="ExternalInput").ap()
    moT = nc.dram_tensor("moT", [1024, T_LAT], BF16, kind="ExternalOutput").ap()
    P = Prog(nc)
    with ExitStack() as st:
        P.setup_mem(st)
        outs = emit_ret(P, nc, hT, wqkg, wv, pv, cs, mk, dr, kdc, gC, moT)
        P.emit(st, final_wait_ops=outs)
    return nc


def emit_ret(P, nc, hT, wqkg, wv, pv, cs, mk, dr, kdc, gC, moT, sfx=""):
    KB = 1024
    qTs = nc.dram_tensor("qTs" + sfx, [128, 4, T_ALL], BF16, kind="Internal").ap()
    kTs = nc.dram_tensor("kTs" + sfx, [128, 4, T_ALL], BF16, kind="Internal").ap()
    Vs = nc.dram_tensor("Vs" + sfx, [T_ALL, 1024], BF16, kind="Internal").ap()
    sgs = nc.dram_tensor("sgs" + sfx, [128, 8, T_LAT], BF16, kind="Internal").ap()
    ofs = nc.dram_tensor("ofs" + sfx, [128, 8, T_LAT], F32, kind="Internal").ap()
    NCH = T_ALL // RC
    qT_c = [Tl(qTs[:, :, c * RC:(c + 1) * RC], f"qTs{c}") for c in range(NCH)]
    kT_c = [Tl(kTs[:, :, c * RC:(c + 1) * RC], f"kTs{c}") for c in range(NCH)]
    V_c = [Tl(Vs[c * RC:(c + 1) * RC, :], f"Vs{c}") for c in range(NCH)]
    sg_c = [Tl(sgs[:, :, c * RC:(c + 1) * RC], f"sg{c}") for c in range(T_LAT // RC)]
    of_c = [Tl(ofs[:, :, c * RC:(c + 1) * RC], f"of{c}") for c in range(T_LAT // RC)]

    C = make_consts(P)
    pvt = P.alloc([128, 2, 3, 16], F32, "pv")
    P.dma(pvt.v(), pv)
    der = P.alloc([128, 2, 16], F32, "der")
    for kd in range(2):
        P.ts(der[:, kd, :], pvt[:, kd, 1, :], 1.0, ALU.add)
        P.tt(der[:, kd, :], der[:, kd, :], pvt[:, kd, 2, :], ALU.mult)
    mkt = P.alloc([128, 2, 2, 128], F32, "mk")
    P.dma(mkt.v(), mk)
    drt = P.alloc([128, 2, 2, 128], F32, "dr")
    P.dma(drt.v(), dr)
    kdt = P.alloc([128, 2, 2], F32, "kdc")
    P.dma(kdt.v(), kdc)
    gct = P.alloc([128, 2], F32, "gC")
    P.dma(gct.v(), gC)
    sq = P.alloc([128, 16, 128], BF16, "sq")
    rstd = P.alloc([128, 128], F32, "rstd")
    base = P.off * 4
    assert base <= 14 * KB, base
    aT = P.alloc_at(14 * KB, [128, 16, T_ALL], BF16, "aT")
    R2 = 153 * KB
    ht = [P.alloc_at(R2 + i * 8 * KB, [128, 16, 128], F32, f"ht{i}") for i in range(2)]
    tmp = P.alloc_at(R2 + 16 * KB, [128, 16, 128], F32, "tmp")
    for si in range(T_ALL // 128):
        a = si * 128
        kd = 1 if a < T_CTX else 0
        h_ = ht[si % 2]
        P.dma(h_.v(), hT[:, a:a + 128].rearrange("(c p) n -> p c n", p=128))
        norm_stats(P, C, h_.v(), 16, 128, sq, rstd, D)
        P.tt(tmp.v(), h_.v(), bc_mid(rstd[:, 0:128], 16), ALU.mult)
        for c in range(16):
            P.act(aT[:, c, a:a + 128], tmp[:, c, :], AF.Identity, scale=der[:, kd, c:c + 1], bias=pvt[:, kd, 0, c:c + 1])
    P.barrier()
    wb = [P.alloc_at(R2 + i * 8 * KB, [128, 16, 256], BF16, f"wb{i}") for i in range(2)]
    cst = [P.alloc_at(R2 + 16 * KB + i * 4 * KB, [128, 2, 512], F32, f"cs{i}") for i in range(2)]
    raw = [P.alloc_at(R2 + 24 * KB + i * 2 * KB, [128, 512], F32, f"raw{i}") for i in range(4)]
    tm = [P.alloc_at(R2 + 32 * KB + i * 2 * KB, [128, 512], F32, f"tm{i}") for i in range(4)]
    ob = [P.alloc_at(R2 + 40 * KB + i * 1 * KB, [128, 512], BF16, f"ob{i}") for i in range(4)]
    ttl = [(0, 256)] + [(256 + i * 512, 512) for i in range(8)]
    cnt = [0]
    for g in range(8):
        w_ = wb[g % 2]
        P.dma(w_.v(), wqkg[g], q="pool")
        for ti, (a, n) in enumerate(ttl):
            is_ctx = a < T_CTX
            if g >= 4 and is_ctx:
                continue
            pss = []
            for mi in range(2):
                ps = P.ps()
                for k in range(16):
                    P.matmul(ps[:, 0:n], w_[:, k, mi * 128:(mi + 1) * 128], aT[:, k, a:a + n], start=(k == 0), stop=(k == 15))
                pss.append(ps)
            if g < 4:
                lh = g % 2
                dst = qTs if g < 2 else kTs
                dst_c = qT_c if g < 2 else kT_c
                scl = 1.0 if g < 2 else 0.0625
                i0 = cnt[0] % 2
                cnt[0] += 1
                r0, r1 = raw[i0 * 2], raw[i0 * 2 + 1]
                o0, o1 = ob[i0 * 2], ob[i0 * 2 + 1]
                wr = [dst_c[c].v() for c in range(a // RC, (a + n) // RC)]
                if is_ctx:
                    P.act(o0[:, 0:n], pss[0][:, 0:n], AF.Identity, scale=scl)
                    P.act(o1[:, 0:n], pss[1][:, 0:n], AF.Identity, scale=scl)
                else:
                    P.act(r0[:, 0:n], pss[0][:, 0:n], AF.Identity, scale=scl)
                    P.act(r1[:, 0:n], pss[1][:, 0:n], AF.Identity, scale=scl)
                    c_ = cst[ti % 2]
                    if g == 0 or True:
                        P.dma(c_[:, :, 0:n], cs[:, :, a - T_CTX:a - T_CTX + n])
                    t0_, t1_, t2_, t3_ = tm[0], tm[1], tm[2], tm[3]
                    P.tt(t0_[:, 0:n], r0[:, 0:n], c_[:, 0, 0:n], ALU.mult)
                    P.tt(t1_[:, 0:n], r1[:, 0:n], c_[:, 1, 0:n], ALU.mult)
                    P.tt(o0[:, 0:n], t0_[:, 0:n], t1_[:, 0:n], ALU.subtract)
                    P.tt(t2_[:, 0:n], r0[:, 0:n], c_[:, 1, 0:n], ALU.mult, eng="pool")
                    P.tt(t3_[:, 0:n], r1[:, 0:n], c_[:, 0, 0:n], ALU.mult, eng="pool")
                    P.tt(o1[:, 0:n], t2_[:, 0:n], t3_[:, 0:n], ALU.add, eng="pool")
                P.dma(dst[:, lh * 2 + 0, a:a + n], o0[:, 0:n], pwrites=wr)
                P.dma(dst[:, lh * 2 + 1, a:a + n], o1[:, 0:n], pwrites=wr)
            else:
                lh = (g - 4) // 2
                vc0 = ((g - 4) % 2) * 2
                i0 = cnt[0] % 2
                cnt[0] += 1
                al = a - T_CTX
                wr = [sg_c[c].v() for c in range(al // RC, (al + n) // RC)]
                for mi in range(2):
                    o_ = ob[i0 * 2 + mi]
                    P.act(o_[:, 0:n], pss[mi][:, 0:n], AF.Silu)
                    P.dma(sgs[:, lh * 4 + vc0 + mi, al:al + n], o_[:, 0:n], pwrites=wr)
    P.barrier()
    wvb = P.alloc_at(R2, [128, 16, 512], BF16, "wvb")
    vst = [P.alloc_at(R2 + 16 * KB + i * KB, [128, 512], BF16, f"vst{i}") for i in range(4)]
    for lh in range(2):
        P.dma(wvb.v(), wv[lh], q="pool")
        for blk in range(NCH):
            ps = P.ps()
            for k in range(16):
                P.matmul(ps[:, 0:512], aT[:, k, blk * 128:(blk + 1) * 128], wvb[:, k, :], start=(k == 0), stop=(k == 15))
            v_ = vst[(lh * NCH + blk) % 4]
            P.copy(v_.v(), ps[:, 0:512], eng="act" if blk % 2 else "dve")
            P.dma(Vs[blk * 128:(blk + 1) * 128, lh * 512:(lh + 1) * 512], v_.v(), pwrites=[V_c[blk].v()])
    P.barrier()
    P.off = base // 4
    NB = 3
    qb = [P.alloc([128, 4, 128], BF16, f"qb{i}") for i in range(NB)]
    kb = [P.alloc([128, 4, 128], BF16, f"kb{i}") for i in range(NB)]
    vb = [P.alloc([128, 1024], BF16, f"vb{i}") for i in range(NB)]
    ofb = [P.alloc([128, 8, 128], F32, f"ofb{i}") for i in range(NB)]
    sgb = [P.alloc([128, 8, 128], BF16, f"sgb{i}") for i in range(NB)]
    S = [P.alloc([128, 2, 512], F32, f"S{lh}") for lh in range(2)]
    Sb = [P.alloc([128, 2, 512], BF16, f"Sb{lh}") for lh in range(2)]
    attb = [P.alloc([128, 128], BF16, f"attb{i}") for i in range(2)]
    qd = [P.alloc([128, 2, 128], BF16, f"qd{i}") for i in range(2)]
    kd_ = [P.alloc([128, 256], BF16, f"kd{i}") for i in range(2)]
    ost = [P.alloc([128, 4, 128], F32, f"ost{i}") for i in range(2)]
    osq = [P.alloc([128, 4, 128], BF16, f"osq{i}") for i in range(2)]
    rs2 = [P.alloc([128, 128], F32, f"rs2{i}") for i in range(2)]
    mst = [P.alloc([128, 4, 128], BF16, f"mst{i}") for i in range(2)]
    outs = []
    for d in range(2):
        for lh in range(2):
            P.memset(S[lh].v(), 0.0)
            P.memset(Sb[lh].v(), 0.0, eng="pool")
        order = [0, 1] + list(range(2, NCH)) if d == 0 else [1, 0] + list(range(NCH - 1, 1, -1))
        for step, ch in enumerate(order):
            is_ctx = ch < 2
            cl = ch - 2
            bi = step % NB
            q_, k_, v_ = qb[bi], kb[bi], vb[bi]
            P.dma(k_.v(), kT_c[ch].v())
            P.dma(v_.v(), V_c[ch].v())
            if not is_ctx:
                P.dma(q_.v(), qT_c[ch].v())
                if d == 1:
                    P.dma(ofb[bi].v(), of_c[cl].v())
                    P.dma(sgb[bi].v(), sg_c[cl].v())
            for lh in range(2):
                u = (step * 2 + lh) % 2
                if not is_ctx:
                    ps_a = P.ps()
                    for kc in range(2):
                        P.matmul(ps_a[:, 0:128], k_[:, lh * 2 + kc, :], q_[:, lh * 2 + kc, :], start=(kc == 0), stop=(kc == 1))
                    P.tt(attb[u].v(), ps_a[:, 0:128], mkt[:, lh, d, :], ALU.mult)
                    P.tt(qd[u].v(), q_[:, lh * 2:lh * 2 + 2, :], bc_mid(drt[:, lh, d, :], 2), ALU.mult, eng="pool")
                    ps_o = P.ps()
                    for vc in range(4):
                        P.matmul(ps_o[:, vc * 128:(vc + 1) * 128], v_[:, lh * 512 + vc * 128:lh * 512 + (vc + 1) * 128], attb[u].v(),
                                 start=True, stop=False)
                        for kc in range(2):
                            P.matmul(ps_o[:, vc * 128:(vc + 1) * 128], Sb[lh][:, kc, vc * 128:(vc + 1) * 128], qd[u][:, kc, :],
                                     start=False, stop=(kc == 1))
                    if d == 0:
                        P.copy(ost[u].v().rearrange("p a b -> p (a b)"), ps_o[:, 0:512], eng="act")
                        P.dma(ofs[:, lh * 4:(lh + 1) * 4, cl * RC:(cl + 1) * RC], ost[u].v(), pwrites=[of_c[cl].v()])
                    else:
                        o_ = ost[u]
                        P.tt(o_.v().rearrange("p a b -> p (a b)"), ps_o[:, 0:512], ofb[bi][:, lh * 4:(lh + 1) * 4, :].rearrange("p a b -> p (a b)"), ALU.add)
                        P.act(osq[u].v(), o_.v(), AF.Square)
                        ps_n = P.ps()
                        for vc in range(4):
                            P.matmul(ps_n[:, 0:128], C["ones_bf"].v(), osq[u][:, vc, :], start=(vc == 0), stop=(vc == 3))
                        P.act(rs2[u].v(), ps_n[:, 0:128], AF.Sqrt, scale=1.0 / 512, bias=float(RMS_EPS))
                        P.recip(rs2[u].v(), rs2[u].v())
                        P.tt(o_.v(), o_.v(), bc_mid(rs2[u].v(), 4), ALU.mult)
                        P.tt(mst[u].v(), o_.v(), sgb[bi][:, lh * 4:(lh + 1) * 4, :], ALU.mult, eng="pool")
                        outs.append(P.dma(moT[lh * 512:(lh + 1) * 512, cl * RC:(cl + 1) * RC].rearrange("(c p) n -> p c n", p=128), mst[u].v()).idx)
                ps_t = P.ps()
                ptb = ps_t.v().bitcast(BF16)
                for kc in range(2):
                    P.transpose(ptb[:, kc * 128:(kc + 1) * 128], k_[:, lh * 2 + kc, :], C["ident_b"].v())
                P.act(kd_[u].v(), ptb[:, 0:256], AF.Identity, scale=kdt[:, lh, d:d + 1])
                for kc in range(2):
                    ps_s = P.ps()
                    P.matmul(ps_s[:, 0:512], kd_[u][:, kc * 128:(kc + 1) * 128], v_[:, lh * 512:(lh + 1) * 512], start=True, stop=True)
                    P.stt(S[lh][:, kc, :], S[lh][:, kc, :], gct[:, lh:lh + 1], ps_s[:, 0:512], ALU.mult, ALU.add)
                    P.copy(Sb[lh][:, kc, :], S[lh][:, kc, :], eng="act")
    return outs


T_CTX = 256
T_LAT = 4096
T_ALL = T_CTX + T_LAT
CH = 64
NCH = T_ALL // CH
C0 = 0.6065306597126334
GN_EPS = 64e-5


def run_interleaved(gens):
    live = list(gens)
    while live:
        nxt = []
        for g in live:
            try:
                next(g)
                nxt.append(g)
            except StopIteration:
                pass
        live = nxt


def build_rg():
    nc = bass.Bass("TRN2", target_bir_lowering=False)
    I = {}
    I["xT"] = nc.dram_tensor("xT", [D, T_ALL], F32, kind="ExternalInput").ap()
    I["wA"] = nc.dram_tensor("wA", [8, 128, 16, 256], F32, kind="ExternalInput").ap()
    I["wvg"] = nc.dram_tensor("wvg", [128, 16, 256], F32, kind="ExternalInput").ap()
    I["pv"] = nc.dram_tensor("pv", [128, 2, 3, 16], F32, kind="ExternalInput").ap()
    I["rp"] = nc.dram_tensor("rp", [128, 32], F32, kind="ExternalInput").ap()
    I["a2"] = nc.dram_tensor("a2", [96, 256], F32, kind="ExternalInput").ap()
    I["w2"] = nc.dram_tensor("w2", [96, 2, 256], F32, kind="ExternalInput").ap()
    I["g2"] = nc.dram_tensor("g2", [128, 2, 256], F32, kind="ExternalInput").ap()
    I["gku"] = nc.dram_tensor("gku", [16, 2, 128], F32, kind="ExternalInput").ap()
    I["bo"] = nc.dram_tensor("bo", [128, 128], F32, kind="ExternalInput").ap()
    I["msk"] = nc.dram_tensor("msk", [128, 2, 3, 128], F32, kind="ExternalInput").ap()
    I["gmk"] = nc.dram_tensor("gmk", [128, 2, 64], F32, kind="ExternalInput").ap()
    I["rst"] = nc.dram_tensor("rst", [128, 2, 512], F32, kind="ExternalInput").ap()
    moT = nc.dram_tensor("moT", [512, T_ALL], BF16, kind="ExternalOutput").ap()
    P = Prog(nc)
    with ExitStack() as st:
        P.setup_mem(st)
        outs = emit_rg(P, nc, I, moT)
        P.emit(st, final_wait_ops=outs)
    return nc


def emit_rg(P, nc, I, moT, sfx=""):
    KB = 1024
    T = T_ALL

    def scratch(name, shape, dt):
        return nc.dram_tensor(name + sfx, shape, dt, kind="Internal").ap()
    zr = scratch("zr", [128, 10, T], F32)
    gq = scratch("gq", [128, T], F32)
    gk = scratch("gk", [128, T], F32)
    gsg = scratch("gsg", [128, 2, T], BF16)
    ggd = scratch("ggd", [16, T], F32)
    Vg = scratch("Vg", [T, 256], BF16)
    sA = [scratch(f"sA{d}", [128, 2, T], F32) for d in range(2)]
    sB = [scratch(f"sB{d}", [128, 2, T], F32) for d in range(2)]
    sK = [scratch(f"sK{d}", [128, 2, T], F32) for d in range(2)]
    sR = [scratch(f"sR{d}", [128, 2, T], F32) for d in range(2)]
    sV = scratch("sV", [128, 2, T], F32)
    sG = scratch("sG", [128, 2, T], F32)
    sBon = scratch("sBon", [128, 2, T], F32)
    gQ = [scratch(f"gQ{d}", [128, T], BF16) for d in range(2)]
    gK = [scratch(f"gK{d}", [128, T], BF16) for d in range(2)]
    NT256 = T // 256

    def trk(name):
        return [Tl(None, f"{name}{i}") for i in range(NT256)]
    tk = {n: trk(n) for n in ("zr", "gl", "Vg", "rw", "gp")}

    C = make_consts(P)
    pvt = P.alloc([128, 2, 3, 16], F32, "pv")
    P.dma(pvt.v(), I["pv"])
    der = P.alloc([128, 2, 16], F32, "der")
    for kd in range(2):
        P.ts(der[:, kd, :], pvt[:, kd, 1, :], 1.0, ALU.add)
        P.tt(der[:, kd, :], der[:, kd, :], pvt[:, kd, 2, :], ALU.mult)
    rp = P.alloc([128, 32], F32, "rp")
    P.dma(rp.v(), I["rp"])
    dr2 = P.alloc([128, 32], F32, "dr2")
    P.ts(dr2[:, 0:10], rp[:, 0:10], -1.0, ALU.mult, 1.0, ALU.add)
    P.ts(dr2[:, 10:20], rp[:, 0:10], 0.5, ALU.mult)
    P.ts(dr2[:, 20:22], rp[:, 18:20], -1.0, ALU.mult, 1.0, ALU.add)
    P.ts(dr2[:, 22:24], rp[:, 28:30], -1.0, ALU.mult)
    bo = P.alloc([128, 128], F32, "bo")
    P.dma(bo.v(), I["bo"])
    msk = P.alloc([128, 2, 3, 128], F32, "msk")
    P.dma(msk.v(), I["msk"])
    gmk = P.alloc([128, 2, 64], F32, "gmk")
    P.dma(gmk.v(), I["gmk"])
    sq = P.alloc([128, 16, 128], BF16, "sq")
    rstd = P.alloc([128, 128], F32, "rstd")
    pCt = P.alloc([128, 2, 2, NCH], F32, "pCt")
    elast = P.alloc([128, 2, NCH], F32, "elast")
    base = P.off * 4
    assert base <= 16 * KB, base
    aT = P.alloc_at(16 * KB, [128, 16, T], BF16, "aT")
    R2 = 155 * KB
    ht = [P.alloc_at(R2 + i * 8 * KB, [128, 16, 128], F32, f"ht{i}") for i in range(2)]
    tmp = P.alloc_at(R2 + 16 * KB, [128, 16, 128], F32, "tmp")
    for si in range(T // 128):
        a = si * 128
        kd = 1 if a < T_CTX else 0
        h_ = ht[si % 2]
        P.dma(h_.v(), I["xT"][:, a:a + 128].rearrange("(c p) n -> p c n", p=128))
        norm_stats(P, C, h_.v(), 16, 128, sq, rstd, D)
        P.tt(tmp.v(), h_.v(), bc_mid(rstd[:, 0:128], 16), ALU.mult)
        for c in range(16):
            P.act(aT[:, c, a:a + 128], tmp[:, c, :], AF.Identity, scale=der[:, kd, c:c + 1], bias=pvt[:, kd, 0, c:c + 1])
    P.barrier()
    wb = [P.alloc_at(R2 + i * 8 * KB, [128, 16, 256], BF16, f"wb{i}") for i in range(2)]
    stf = [P.alloc_at(R2 + 16 * KB + i * 2 * KB, [128, 512], F32, f"stf{i}") for i in range(4)]
    stb = [P.alloc_at(R2 + 24 * KB + i * KB, [128, 512], BF16, f"stb{i}") for i in range(4)]
    ttl = [(0, 256)] + [(256 + i * 512, 512) for i in range(8)]
    cnt = [0]
    for g in range(8):
        w_ = wb[g % 2]
        P.dma(w_.v(), I["wA"][g], q="pool")
        for mi in range(2):
            ch = g * 2 + mi
            if ch >= 15:
                continue
            for ti, (a, n) in enumerate(ttl):
                ps = P.ps()
                for k in range(16):
                    P.matmul(ps[:, 0:n], w_[:, k, mi * 128:(mi + 1) * 128], aT[:, k, a:a + n], start=(k == 0), stop=(k == 15))
                i0 = cnt[0] % 4
                cnt[0] += 1
                e_ = "act" if cnt[0] % 2 else "dve"
                tks = range(a // 256, (a + n) // 256)
                if ch < 10:
                    P.copy(stf[i0][:, 0:n], ps[:, 0:n], eng=e_)
                    P.dma(zr[:, ch, a:a + n], stf[i0][:, 0:n], pwrites=[tk["zr"][i] for i in tks])
                elif ch == 10:
                    P.act(stf[i0][:, 0:n], ps[:, 0:n], AF.Identity, scale=float(128 ** -0.5))
                    P.dma(gq[:, a:a + n], stf[i0][:, 0:n], pwrites=[tk["gl"][i] for i in tks])
                elif ch == 11:
                    P.copy(stf[i0][:, 0:n], ps[:, 0:n], eng=e_)
                    P.dma(gk[:, a:a + n], stf[i0][:, 0:n], pwrites=[tk["gl"][i] for i in tks])
                elif ch in (12, 13):
                    P.act(stb[i0][:, 0:n], ps[:, 0:n], AF.Silu)
                    P.dma(gsg[:, ch - 12, a:a + n], stb[i0][:, 0:n], pwrites=[tk["gl"][i] for i in tks])
                else:
                    P.copy(stf[i0][0:16, 0:n], ps[0:16, 0:n], eng=e_)
                    P.dma(ggd[:, a:a + n], stf[i0][0:16, 0:n], pwrites=[tk["gl"][i] for i in tks])
    P.barrier()
    wvb = P.alloc_at(R2, [128, 16, 256], BF16, "wvb")
    vst = [P.alloc_at(R2 + 8 * KB + i * KB, [128, 256], BF16, f"vst{i}") for i in range(4)]
    P.dma(wvb.v(), I["wvg"], q="pool")
    for blk in range(T // 128):
        ps = P.ps()
        for k in range(16):
            P.matmul(ps[:, 0:256], aT[:, k, blk * 128:(blk + 1) * 128], wvb[:, k, :], start=(k == 0), stop=(k == 15))
        v_ = vst[blk % 4]
        P.copy(v_.v(), ps[:, 0:256], eng="act" if blk % 2 else "dve")
        P.dma(Vg[blk * 128:(blk + 1) * 128, :], v_.v(), pwrites=[tk["Vg"][blk // 2]])
    P.barrier()
    P.off = base // 4
    a2t = P.alloc([96, 256], F32, "a2t")
    P.dma(a2t.v(), I["a2"])
    w2t = P.alloc([96, 2, 256], F32, "w2t")
    P.dma(w2t.v(), I["w2"])
    g2t = P.alloc([128, 2, 256], F32, "g2t")
    P.dma(g2t.v(), I["g2"])
    gkut = P.alloc([16, 2, 128], F32, "gkut")
    P.dma(gkut.v(), I["gku"])
    rst = P.alloc([128, 2, 512], F32, "rst")
    P.dma(rst.v(), I["rst"])
    n = 256
    zt = [P.alloc([128, 10, n + 2], F32, f"zt{i}") for i in range(2)]
    tsum = P.alloc([128, 10, n], F32, "tsum")
    zs = P.alloc([128, 10, n], F32, "zs")
    lr = P.alloc([128, 2, n], F32, "lr")
    tw = P.alloc([96, n], F32, "tw")
    sig = P.alloc([128, 2, 2, n], F32, "sig")
    sgd = P.alloc([128, 2, n], F32, "sgd")
    gst = P.alloc([128, 2, n], F32, "gst")
    kk = P.alloc([128, 2, n], F32, "kk")
    ksq = P.alloc([128, 2, n], F32, "ksq")
    rinv = P.alloc([128, 2, n], F32, "rinv")
    fac = P.alloc([128, 2, n], F32, "fac")
    kp = P.alloc([128, 2, n], F32, "kp")
    bb = P.alloc([128, 2, n], F32, "bb")
    rk = P.alloc([128, 2, n], F32, "rk")
    bon = P.alloc([128, 2, n], F32, "bon")
    cs = P.alloc([128, 2, n], F32, "cs")
    e_inv = P.alloc([128, 2, n], F32, "e_inv")
    e_inc = P.alloc([128, 2, n], F32, "e_inc")
    e_exc = P.alloc([128, 2, n], F32, "e_exc")
    oA = [P.alloc([128, 2, n], F32, f"oA{i}") for i in range(2)]
    oB = [P.alloc([128, 2, n], F32, f"oB{i}") for i in range(2)]
    oK = [P.alloc([128, 2, n], F32, f"oK{i}") for i in range(2)]
    oR = [P.alloc([128, 2, n], F32, f"oR{i}") for i in range(2)]
    gdt = P.alloc([16, n], F32, "gdt")
    gqt = P.alloc([128, n], F32, "gqt")
    gkt = P.alloc([128, n], F32, "gkt")
    gsp = P.alloc([128, n], F32, "gsp")
    gcs = P.alloc([128, n], F32, "gcs")
    ge1 = P.alloc([128, n], F32, "ge1")
    ge2 = P.alloc([128, n], F32, "ge2")
    gqo = [P.alloc([128, n], BF16, f"gqo{i}") for i in range(2)]
    gko = [P.alloc([128, n], BF16, f"gko{i}") for i in range(2)]

    def fl(v):
        return v.rearrange("p a b -> p (a b)")
    for ti in range(NT256):
        a = ti * n
        z_ = zt[ti % 2]
        lz = (a == 0) or (a == T_CTX)
        rz = (a + n == T_CTX) or (a + n == T)
        rd = [tk["zr"][i] for i in range(max(0, ti - 1), min(NT256, ti + 2))]
        if lz:
            P.memset(z_[:, :, 0:1], 0.0, eng="pool")
        if rz:
            P.memset(z_[:, :, n + 1:n + 2], 0.0, eng="pool")
        lo = a if lz else a - 1
        hi = a + n if rz else a + n + 1
        P.dma(z_[:, :, (lo - a + 1):(hi - a + 1)], zr[:, :, lo:hi], reads=rd)
        P.tt(tsum.v(), z_[:, :, 0:n], z_[:, :, 2:n + 2], ALU.add, eng="pool")
        P.tt(tsum.v(), tsum.v(), bc_inner(dr2[:, 10:20], n), ALU.mult, eng="pool")
        P.tt(zs.v(), z_[:, :, 1:n + 1], bc_inner(dr2[:, 0:10], n), ALU.mult)
        P.tt(zs.v(), zs.v(), tsum.v(), ALU.add)
        wrk = [tk["rw"][ti]]
        P.dma(sV[:, :, a:a + n], zs[:, 4:6, :], pwrites=wrk)
        for c in range(2):
            ps = P.ps()
            P.matmul(ps[:, 0:n], a2t[0:96, c * 128:(c + 1) * 128], zs[0:96, 7, :])
            P.act(lr[:, c, :], ps[:, 0:n], AF.Sigmoid, bias=rp[:, 10 + c:11 + c])
        P.act(tw.v(), zs[0:96, 6, :], AF.Tanh)
        for d in range(2):
            for c in range(2):
                ps = P.ps()
                P.matmul(ps[:, 0:n], w2t[0:96, d, c * 128:(c + 1) * 128], tw.v())
                P.act(sig[:, d, c, :], ps[:, 0:n], AF.Sigmoid, bias=rp[:, 12 + d * 2 + c:13 + d * 2 + c])
        P.act(sgd.v(), zs[:, 8:10, :], AF.Sigmoid)
        for c in range(2):
            ps = P.ps()
            for kc in range(2):
                P.matmul(ps[:, 0:n], g2t[:, kc, c * 128:(c + 1) * 128], sgd[:, kc, :], start=(kc == 0), stop=(kc == 1))
            P.copy(gst[:, c, :], ps[:, 0:n], eng="act")
        P.dma(sG[:, :, a:a + n], gst.v(), pwrites=wrk)
        P.tt(kk.v(), zs[:, 2:4, :], bc_inner(rp[:, 16:18], n), ALU.mult)
        P.act(ksq.v(), kk.v(), AF.Square)
        for c in range(2):
            ps = P.ps()
            P.matmul(ps[:, 0:n], bo.v(), ksq[:, c, :])
            P.act(rinv[:, c, :], ps[:, 0:n], AF.Sqrt)
        P.ts(rinv.v(), rinv.v(), 1e-12, ALU.max)
        P.recip(fl(rinv.v()), fl(rinv.v()))
        P.tt(kk.v(), kk.v(), rinv.v(), ALU.mult)
        for c in range(2):
            P.ts(fac[:, c, :], lr[:, c, :], rp[:, 18 + c:19 + c], ALU.mult, dr2[:, 20 + c:21 + c], ALU.add)
        P.tt(kp.v(), zs[:, 2:4, :], fac.v(), ALU.mult)
        P.tt(bb.v(), kk.v(), lr.v(), ALU.mult)
        P.tt(rk.v(), zs[:, 0:2, :], kp.v(), ALU.mult, eng="pool")
        P.tt(rk.v(), rk.v(), bc_inner(rp[:, 20:22], n), ALU.mult, eng="pool")
        for c in range(2):
            ps = P.ps()
            P.matmul(ps[:, 0:n], bo.v(), rk[:, c, :])
            P.tt(bon[:, c, :], ps[:, 0:n], zs[:, 4 + c, :], ALU.mult)
        P.dma(sBon[:, :, a:a + n], bon.v(), pwrites=wrk)
        nch = n // CH
        ch0 = a // CH
        for d in range(2):
            sgv = fl(sig[:, d, :, :])
            if d == 0:
                P.scan(fl(cs.v()), rst[:, 0, 0:2 * n], sgv, 0.0, ALU.mult, ALU.add)
            else:
                co = fl(cs.v()).ap[:, ::-1]
                si_ = sgv.ap[:, ::-1]
                rs_ = rst.ap[:, 1, 0:2 * n][:, ::-1]
                P.raw("dve", lambda e, co=co, si_=si_, rs_=rs_: e.tensor_tensor_scan(co, rs_, si_, 0.0, ALU.mult, ALU.add),
                      [sig.v(), rst.v()], [cs.v()])
            P.act(e_inv.v(), cs.v(), AF.Exp, scale=C0)
            P.act(e_inc.v(), cs.v(), AF.Exp, scale=-C0)
            P.tt(e_exc.v(), cs.v(), sig[:, d, :, :], ALU.subtract, eng="pool")
            P.act(e_exc.v(), e_exc.v(), AF.Exp, scale=-C0)
            o_ = d
            P.stt(oA[o_].v(), kk.v(), -1.0, e_exc.v(), ALU.mult, ALU.mult)
            P.tt(oB[o_].v(), bb.v(), e_inv.v(), ALU.mult)
            P.tt(oK[o_].v(), kp.v(), e_inv.v(), ALU.mult, eng="pool")
            P.tt(oR[o_].v(), zs[:, 0:2, :], e_inc.v(), ALU.mult, eng="pool")
            col = CH - 1 if d == 0 else 0
            P.copy(pCt[:, d, :, ch0:ch0 + nch], e_inc.v().rearrange("p c (k t) -> p c k t", t=CH)[:, :, :, col], eng="act")
            P.dma(sA[d][:, :, a:a + n], oA[o_].v(), pwrites=wrk)
            P.dma(sB[d][:, :, a:a + n], oB[o_].v(), pwrites=wrk)
            P.dma(sK[d][:, :, a:a + n], oK[o_].v(), pwrites=wrk)
            P.dma(sR[d][:, :, a:a + n], oR[o_].v(), pwrites=wrk)
        rdg = [tk["gl"][ti]]
        P.dma(gdt.v(), ggd[:, a:a + n], reads=rdg)
        P.dma(gqt.v(), gq[:, a:a + n], reads=rdg)
        P.dma(gkt.v(), gk[:, a:a + n], reads=rdg)
        for d in range(2):
            ps = P.ps()
            P.matmul(ps[:, 0:n], gkut[0:16, d, :], gdt.v())
            P.act(gsp.v(), ps[:, 0:n], AF.Exp, scale=-1.0, bias=dr2[:, 22 + d:23 + d])
            P.act(gsp.v(), gsp.v(), AF.Ln, bias=1.0)
            if d == 0:
                P.scan(gcs.v(), rst[:, 0, 0:n], gsp.v(), 0.0, ALU.mult, ALU.add)
            else:
                co = gcs.ap[:, ::-1]
                si_ = gsp.ap[:, ::-1]
                rs_ = rst.ap[:, 1, 0:n][:, ::-1]
                P.raw("dve", lambda e, co=co, si_=si_, rs_=rs_: e.tensor_tensor_scan(co, rs_, si_, 0.0, ALU.mult, ALU.add),
                      [gsp.v(), rst.v()], [gcs.v()])
            P.act(ge1.v(), gcs.v(), AF.Exp, scale=-1.0 / 16)
            P.act(ge2.v(), gcs.v(), AF.Exp, scale=1.0 / 16)
            P.tt(gqo[d].v(), gqt.v(), ge1.v(), ALU.mult)
            P.tt(gko[d].v(), gkt.v(), ge2.v(), ALU.mult, eng="pool")
            col = CH - 1 if d == 0 else 0
            P.copy(elast[:, d, ch0:ch0 + nch], ge1.v().rearrange("p (k t) -> p k t", t=CH)[:, :, col], eng="act")
            P.dma(gQ[d][:, a:a + n], gqo[d].v(), pwrites=[tk["gp"][ti]])
            P.dma(gK[d][:, a:a + n], gko[d].v(), pwrites=[tk["gp"][ti]])
    P.barrier()
    P.off = base // 4
    off_e = P.off
    ya = P.alloc([128, 2, T], F32, "ya")
    P.memset(ya[:, 0, :], 0.0)
    P.memset(ya[:, 1, :], 0.0, eng="pool")
    P.barrier()
    ya_c = [[Tl(ya.ap[:, p, c * CH:(c + 1) * CH], f"ya{p}_{c}") for c in range(NCH)] for p in range(2)]
    idf = C["ident_f"]
    NB = 2
    slots = []
    for s in range(4):
        sl = {}
        for gen in range(NB):
            for nm in ("Ablk", "Bblk", "Kblk", "Vblk"):
                t = P.alloc([128, 128], F32, f"{nm}{s}{gen}")
                P.memset(t.v(), 0.0, eng="pool" if (gen % 2) else "dve")
                sl[(nm, gen)] = t
            for nm in ("A", "B", "R"):
                sl[(nm, gen)] = P.alloc([128, CH], F32, f"{nm}{s}{gen}")
        for nm in ("S0", "S0T", "LakT", "Atb", "Btb", "Ktb", "Vtb", "Ta", "Tb", "Sa", "SaT", "Sb", "SbT",
                   "W1", "Xtb", "Ahat", "GpT", "Hadd"):
            sl[nm] = P.alloc([128, 128], F32, f"{nm}{s}")
        for nm in ("ArbT", "ArkT", "Rhat"):
            sl[nm] = P.alloc([128, CH], F32, f"{nm}{s}")
        sl["H"] = P.alloc([128, 128], F32, f"H{s}")
        slots.append(sl)
    orders = [list(range(NCH)), [3, 2, 1, 0] + list(range(NCH - 1, 3, -1))]

    def unit(s, pair, d, step, ch):
        sl = slots[s]
        gen = step % NB
        Ablk, Bblk, Kblk, Vblk = (sl[(nm, gen)] for nm in ("Ablk", "Bblk", "Kblk", "Vblk"))
        A_, B_, R_ = (sl[(nm, gen)] for nm in ("A", "B", "R"))
        tsl = slice(ch * CH, (ch + 1) * CH)
        rd = [tk["rw"][ch // 4]]
        for (blk, src) in ((Ablk, sA[d]), (Bblk, sB[d]), (Kblk, sK[d]), (Vblk, sV)):
            P.dma(blk[0:64, 0:64], src[0:64, pair, tsl], reads=rd)
            P.dma(blk[64:128, 64:128], src[64:128, pair, tsl], reads=rd)
        P.dma(A_.v(), sA[d][:, pair, tsl], reads=rd)
        P.dma(B_.v(), sB[d][:, pair, tsl], reads=rd)
        P.dma(R_.v(), sR[d][:, pair, tsl], reads=rd)
        yield
        Ms, MsT, MiT = msk[:, d, 0, :], msk[:, d, 1, :], msk[:, d, 2, 0:CH]

        def mm_ev(dst, lhsT, rhs, N, mask=None, eng="dve", add=None):
            ps = P.ps()
            P.matmul(ps[:, 0:N], lhsT, rhs)
            if mask is not None:
                P.tt(dst, ps[:, 0:N], mask, ALU.mult, eng="dve")
            elif add is not None:
                P.tt(dst, ps[:, 0:N], add, ALU.add, eng="dve")
            else:
                P.copy(dst, ps[:, 0:N], eng=eng)
        def dup(t):
            ap = t.ap
            (ps_, pn), (s1, n1) = ap.ap
            return Vw(t, bass.AP(ap.tensor, ap.offset, [[ps_, pn], [0, 2], [s1, n1]]))
        Ad, Bd = dup(A_), dup(B_)
        ps = P.ps()
        P.matmul(ps[:, 0:128].rearrange("p (a b) -> p a b", a=2), Bblk.v(), Ad)
        P.tt(sl["S0T"].v(), ps[:, 0:128], MsT, ALU.mult)
        ps = P.ps()
        P.matmul(ps[:, 0:128].rearrange("p (a b) -> p a b", a=2), Ablk.v(), Bd)
        P.tt(sl["S0"].v(), ps[:, 0:128], Ms, ALU.mult)
        yield
        ps = P.ps()
        P.matmul(ps[:, 0:128].rearrange("p (a b) -> p a b", a=2), Kblk.v(), Ad)
        P.tt(sl["LakT"].v(), ps[:, 0:128], MsT, ALU.mult)
        mm_ev(sl["ArbT"].v(), Bblk.v(), R_.v(), CH, mask=MiT)
        mm_ev(sl["ArkT"].v(), Kblk.v(), R_.v(), CH, mask=MiT)
        yield
        for (dst, src) in (("Atb", Ablk), ("Btb", Bblk), ("Ktb", Kblk), ("Vtb", Vblk)):
            ps = P.ps()
            P.transpose(ps[:, 0:128], src.v(), idf.v())
            P.copy(sl[dst].v(), ps[:, 0:128], eng="act")
        P.tt(sl["Ta"].v(), sl["S0T"].v(), idf.v(), ALU.add, eng="pool")
        yield
        S, ST = sl["S0"], sl["S0T"]
        Tcur, Tnxt = sl["Ta"], sl["Tb"]
        pp = [(sl["Sa"], sl["SaT"]), (sl["Sb"], sl["SbT"])]
        for lvl in range(5):
            Sn, SnT = pp[lvl % 2]
            mm_ev(Sn.v(), ST.v(), S.v(), 128, eng="act")
            if lvl < 4:
                mm_ev(SnT.v(), S.v(), ST.v(), 128, eng="act")
            yield
            mm_ev(Tnxt.v(), Sn.v(), Tcur.v(), 128, add=Tcur.v())
            Tcur, Tnxt = Tnxt, Tcur
            S, ST = Sn, SnT
            yield
        TT = Tcur
        mm_ev(sl["W1"].v(), sl["LakT"].v(), sl["Vtb"].v(), 128, eng="act")
        mm_ev(sl["Ahat"].v(), TT.v(), sl["Atb"].v(), 128, eng="act")
        yield
        mm_ev(sl["Xtb"].v(), TT.v(), sl["W1"].v(), 128, eng="act")
        mm_ev(sl["GpT"].v(), sl["Ahat"].v(), sl["Btb"].v(), 128, add=idf.v())
        mm_ev(sl["Rhat"].v(), sl["Ahat"].v(), sl["ArbT"].v(), CH, add=R_.v())
        yield
        pc = pCt[:, d, pair, ch:ch + 1]
        ps = P.ps()
        P.matmul(ps[:, 0:128], sl["Btb"].v(), sl["Xtb"].v(), start=True, stop=False)
        P.matmul(ps[:, 0:128], sl["Ktb"].v(), sl["Vtb"].v(), start=False, stop=True)
        P.ts(sl["Hadd"].v(), ps[:, 0:128], pc, ALU.mult)
        yield
        H = sl["H"]
        ps = P.ps()
        P.matmul(ps[:, 0:CH], sl["Vtb"].v(), sl["ArkT"].v(), start=True, stop=False)
        P.matmul(ps[:, 0:CH], sl["Xtb"].v(), sl["ArbT"].v(), start=False, stop=False)
        P.matmul(ps[:, 0:CH], H.v(), sl["Rhat"].v(), start=False, stop=True)
        yc = ya_c[pair][ch]
        P.tt(yc.v(), yc.v(), ps[:, 0:CH], ALU.add)
        ps = P.ps()
        P.matmul(ps[:, 0:128], sl["GpT"].v(), H.v())
        P.stt(H.v(), ps[:, 0:128], pc, sl["Hadd"].v(), ALU.mult, ALU.add)
        yield

    for s in range(4):
        P.memset(slots[s]["H"].v(), 0.0)
    for step in range(NCH):
        gens = []
        for s, (pair, d) in enumerate(((0, 0), (1, 0), (0, 1), (1, 1))):
            gens.append(unit(s, pair, d, step, orders[d][step]))
        run_interleaved(gens)
    P.barrier()
    P.off = off_e + (2 * T * 4 // 4 + 7) // 8 * 8
    yb = P.alloc([128, 2, T], F32, "yb")
    P.memset(yb[:, 0, :], 0.0)
    P.memset(yb[:, 1, :], 0.0, eng="pool")
    P.barrier()
    yb_c = [Tl(yb.ap[:, :, c * CH:(c + 1) * CH], f"yb{c}") for c in range(NCH)]
    NBG = 3
    gqb = [[P.alloc([128, CH], BF16, f"gqb{d}{i}") for i in range(NBG)] for d in range(2)]
    gkb_ = [[P.alloc([128, CH], BF16, f"gkb{d}{i}") for i in range(NBG)] for d in range(2)]
    gvb = [[P.alloc([CH, 256], BF16, f"gvb{d}{i}") for i in range(NBG)] for d in range(2)]
    gS = [P.alloc([128, 256], F32, f"gS{d}") for d in range(2)]
    gSb = [P.alloc([128, 256], BF16, f"gSb{d}") for d in range(2)]
    gat = [P.alloc([CH, CH], BF16, f"gat{d}") for d in range(2)]
    gkt_ = [P.alloc([CH, 128], BF16, f"gkt{d}") for d in range(2)]
    gtm = [P.alloc([128, 256], F32, f"gtm{d}") for d in range(2)]
    idb = C["ident_b"]

    def gunit(d, step, ch):
        bi = step % NBG
        q_, k_, v_ = gqb[d][bi], gkb_[d][bi], gvb[d][bi]
        tsl = slice(ch * CH, (ch + 1) * CH)
        rd = [tk["gp"][ch // 4]]
        P.dma(q_.v(), gQ[d][:, tsl], reads=rd)
        P.dma(k_.v(), gK[d][:, tsl], reads=rd)
        P.dma(v_.v(), Vg[tsl, :], reads=[tk["Vg"][ch // 4]])
        yield
        ps = P.ps()
        P.matmul(ps[0:CH, 0:CH], k_.v(), q_.v())
        P.tt(gat[d].v(), ps[0:CH, 0:CH], gmk[0:CH, d, :], ALU.mult)
        pst = P.ps()
        ptb = pst.v().bitcast(BF16)
        P.transpose(ptb[0:CH, 0:128], k_.v(), idb.v())
        P.copy(gkt_[d].v(), ptb[0:CH, 0:128], eng="act")
        yield
        pso = P.ps()
        for vc in range(2):
            P.matmul(pso[:, vc * CH:(vc + 1) * CH], v_[:, vc * 128:(vc + 1) * 128], gat[d].v(), start=True, stop=False)
            P.matmul(pso[:, vc * CH:(vc + 1) * CH], gSb[d][:, vc * 128:(vc + 1) * 128], q_.v(), start=False, stop=True)
        yc = yb_c[ch]
        P.tt(yc.v(), yc.v(), pso[:, 0:2 * CH].rearrange("p (a b) -> p a b", a=2), ALU.add)
        yield
        pss = P.ps()
        P.matmul(pss[:, 0:256], gkt_[d].v(), v_.v())
        el = elast[:, d, ch:ch + 1]
        P.ts(gtm[d].v(), gS[d].v(), el, ALU.mult, eng="pool")
        P.stt(gS[d].v(), pss[:, 0:256], el, gtm[d].v(), ALU.mult, ALU.add)
        P.copy(gSb[d].v(), gS[d].v(), eng="act")
        yield
    for d in range(2):
        P.memset(gS[d].v(), 0.0)
        P.memset(gSb[d].v(), 0.0, eng="pool")
    for step in range(NCH):
        run_interleaved([gunit(d, step, orders[d][step]) for d in range(2)])
    P.barrier()
    n = 256
    fg = [P.alloc([128, 2, n], F32, f"fg{i}") for i in range(2)]
    fbn = [P.alloc([128, 2, n], F32, f"fbn{i}") for i in range(2)]
    fsg = [P.alloc([128, 2, n], BF16, f"fsg{i}") for i in range(2)]
    ysq = P.alloc([128, 2, n], F32, "ysq")
    mean = P.alloc([128, 2, n], F32, "mean")
    var = P.alloc([128, 2, n], F32, "var")
    yn = P.alloc([128, 2, n], F32, "yn")
    oa = [P.alloc([128, 2, n], BF16, f"oa{i}") for i in range(2)]
    bsq = P.alloc([128, 2, n], BF16, "bsq")
    brs = P.alloc([128, n], F32, "brs")
    ybn = P.alloc([128, 2, n], F32, "ybn")
    obb = [P.alloc([128, 2, n], BF16, f"obb{i}") for i in range(2)]
    outs = []
    for ti in range(NT256):
        a = ti * n
        i2 = ti % 2
        P.dma(fg[i2].v(), sG[:, :, a:a + n], reads=[tk["rw"][ti]])
        P.dma(fbn[i2].v(), sBon[:, :, a:a + n], reads=[tk["rw"][ti]])
        P.dma(fsg[i2].v(), gsg[:, :, a:a + n], reads=[tk["gl"][ti]])
        yv = ya[:, :, a:a + n]
        P.act(ysq.v(), yv, AF.Square)
        for c in range(2):
            ps = P.ps()
            P.matmul(ps[:, 0:n], bo.v(), ya[:, c, a:a + n])
            P.act(mean[:, c, :], ps[:, 0:n], AF.Identity, scale=1.0 / 64)
            ps = P.ps()
            P.matmul(ps[:, 0:n], bo.v(), ysq[:, c, :])
            P.act(var[:, c, :], ps[:, 0:n], AF.Identity, scale=1.0 / 64)
        P.tt(yn.v(), mean.v(), mean.v(), ALU.mult, eng="pool")
        P.tt(var.v(), var.v(), yn.v(), ALU.subtract, eng="pool")
        P.act(var.v(), var.v(), AF.Sqrt, bias=float(GN_EPS))
        P.recip(fl(var.v()), fl(var.v()))
        P.tt(yn.v(), yv, mean.v(), ALU.subtract)
        P.tt(yn.v(), yn.v(), var.v(), ALU.mult)
        for c in range(2):
            P.ts(yn[:, c, :], yn[:, c, :], rp[:, 22 + c:23 + c], ALU.mult, rp[:, 24 + c:25 + c], ALU.add)
        P.tt(yn.v(), yn.v(), fbn[i2].v(), ALU.add)
        P.tt(oa[i2].v(), yn.v(), fg[i2].v(), ALU.mult)
        outs.append(P.dma(moT[0:256, a:a + n].rearrange("(c p) n -> p c n", p=128), oa[i2].v()).idx)
        ybv = yb[:, :, a:a + n]
        P.act(bsq.v(), ybv, AF.Square)
        ps = P.ps()
        for c in range(2):
            P.matmul(ps[:, 0:n], C["ones_bf"].v(), bsq[:, c, :], start=(c == 0), stop=(c == 1))
        P.act(brs.v(), ps[:, 0:n], AF.Sqrt, scale=1.0 / 256, bias=float(RMS_EPS))
        P.recip(brs.v(), brs.v())
        P.tt(ybn.v(), ybv, bc_mid(brs.v(), 2), ALU.mult)
        for c in range(2):
            P.stt(obb[i2][:, c, :], ybn[:, c, :], rp[:, 26 + c:27 + c], fsg[i2][:, c, :], ALU.mult, ALU.mult)
        outs.append(P.dma(moT[256:512, a:a + n].rearrange("(c p) n -> p c n", p=128), obb[i2].v()).idx)
    return outs


BF = ml_dtypes.bfloat16

def tile_w(W, ncg):
    K, Cc = W.shape
    KC = K // 128
    G = Cc // ncg
    return np.ascontiguousarray(W.reshape(KC, 128, G, ncg).transpose(2, 1, 0, 3))

def vecT(v):
    n = v.shape[-1] // 128
    return np.ascontiguousarray(v.reshape(n, 128).T)

def tile_wi(W):
    Wg, Wu = W[:, :5632], W[:, 5632:]
    cat = np.concatenate([Wg.reshape(2048, 22, 256), Wu.reshape(2048, 22, 256)], axis=2).reshape(2048, 22 * 512)
    return tile_w(cat, 512)

def post_pv(mod3, ng, b):
    out = np.zeros((128, 2, 7, 16), np.float32)
    for kd, row in enumerate((b, 2)):
        m = mod3[row].reshape(6, 2048)
        vs = [m[2], m[3], m[4], m[5], ng[1], ng[2], ng[3]]
        for i, v in enumerate(vs):
            out[:, kd, i, :] = vecT(v)
    return out

RET_GAMMA = [1.0 - 2.0 ** (-5.0 - h) for h in range(8)]

def ret_consts(hp, C=128):
    mk = np.zeros((128, 2, 2, C), np.float32)
    dr = np.zeros((128, 2, 2, C), np.float32)
    kdc = np.zeros((128, 2, 2), np.float32)
    gC = np.zeros((128, 2), np.float32)
    idx = np.arange(C)
    for lh in range(2):
        g = np.float64(np.float32(RET_GAMMA[2 * hp + lh]))
        lg = np.log(np.float32(g)).astype(np.float64)
        diff = idx[None, :] - idx[:, None]
        mk[:, lh, 0, :] = np.where(diff >= 0, np.exp(diff * lg), 0.0)
        mk[:, lh, 1, :] = np.where(diff <= 0, np.exp(-diff * lg), 0.0)
        dr[:, lh, 0, :] = np.exp((idx + 1.0) * lg)[None, :]
        dr[:, lh, 1, :] = np.exp((C - idx) * lg)[None, :]
        kdc[:, lh, 0] = np.exp((C - 1.0 - idx) * lg)
        kdc[:, lh, 1] = np.exp(idx * lg)
        gC[:, lh] = np.exp(C * lg)
    return mk, dr, kdc, gC

def rot_tables():
    nf = 64
    inv = (10000.0 ** (-np.arange(nf, dtype=np.float32) / nf)).astype(np.float32)
    rows = np.repeat(np.arange(64, dtype=np.float32), 64)
    cols = np.tile(np.arange(64, dtype=np.float32), 64)
    ang = np.concatenate([rows[:, None] * inv, cols[:, None] * inv], axis=-1).astype(np.float32)
    cs = np.stack([np.cos(ang.astype(np.float64)).T, np.sin(ang.astype(np.float64)).T], 1).astype(np.float32)
    return np.ascontiguousarray(cs)

def ret_inputs(hT_b, w_in, mod3, ng, b, hp, cs):
    h0, h1 = 2 * hp, 2 * hp + 1
    sel = []
    for h in (h0, h1):
        sel.append(w_in[:, h * 256:(h + 1) * 256])
    for h in (h0, h1):
        sel.append(w_in[:, 2048 + h * 256:2048 + (h + 1) * 256])
    for h in (h0, h1):
        sel.append(w_in[:, 8192 + h * 512:8192 + h * 512 + 256])
        sel.append(w_in[:, 8192 + h * 512 + 256:8192 + (h + 1) * 512])
    wqkg = np.stack([tile_w(s, 256)[0] for s in sel], 0)
    wv = np.stack([tile_w(w_in[:, 4096 + h * 512:4096 + (h + 1) * 512], 512)[0] for h in (h0, h1)], 0)
    pv = np.zeros((128, 2, 3, 16), np.float32)
    for kd, row in enumerate((b, 2)):
        m = mod3[row].reshape(6, 2048)
        for i, v in enumerate((m[0], m[1], ng[0])):
            pv[:, kd, i, :] = vecT(v)
    mk, dr, kdc, gC = ret_consts(hp)
    return {"hT": hT_b, "wqkg": wqkg, "wv": wv, "pv": pv, "cs": cs, "mk": mk, "dr": dr, "kdc": kdc, "gC": gC}

def rg_consts():
    bo = np.zeros((128, 128), np.float32)
    bo[:64, :64] = 1; bo[64:, 64:] = 1
    msk = np.zeros((128, 2, 3, 128), np.float32)
    i = np.arange(64)
    lt = (i[:, None] > i[None, :]).astype(np.float32)
    le = (i[:, None] <= i[None, :]).astype(np.float32)
    for h in range(2):
        sl = slice(h * 64, (h + 1) * 64)
        msk[sl, 0, 0, sl] = lt
        msk[sl, 0, 1, sl] = lt.T
        msk[sl, 0, 2, 0:64] = le
        msk[sl, 1, 0, sl] = lt.T
        msk[sl, 1, 1, sl] = lt
        msk[sl, 1, 2, 0:64] = le.T
    gmk = np.zeros((128, 2, 64), np.float32)
    gmk[:64, 0, :] = le
    gmk[:64, 1, :] = le.T
    rst = np.ones((128, 2, 512), np.float32)
    rst[:, 0, 0::64] = 0
    rst[:, 1, 63::64] = 0
    return {"bo": bo, "msk": msk, "gmk": gmk, "rst": rst}

def pad128(w):
    if w.shape[1] == 128:
        return w
    o = np.zeros((w.shape[0], 128), np.float32)
    o[:, :w.shape[1]] = w
    return o

def vec2(v):
    return np.ascontiguousarray(v.reshape(2, 128).T)

def vecpad(v):
    o = np.zeros(128, np.float32)
    o[:v.shape[0]] = v
    return o

def rg_inputs(xT_b, inp, mod3, ng, b, g, consts):
    w = inp["ab_w_in"][0]
    A0 = 3520
    o = 256 * g
    ch = []
    for base in (0, 1024, 2048):
        ch += [w[:, base + o:base + o + 128], w[:, base + o + 128:base + o + 256]]
    ch.append(pad128(w[:, 3072:3168]))
    ch.append(pad128(w[:, 3168:3264]))
    ch += [w[:, 3264:3392], w[:, 3392:3520]]
    ch.append(w[:, A0 + 128 * g:A0 + 128 * g + 128])
    ch.append(w[:, A0 + 512 + 128 * g:A0 + 512 + 128 * g + 128])
    ch += [w[:, A0 + 2048 + o:A0 + 2048 + o + 128], w[:, A0 + 2048 + o + 128:A0 + 2048 + o + 256]]
    ch.append(pad128(w[:, A0 + 3072:A0 + 3088]))
    ch.append(np.zeros((2048, 128), np.float32))
    wA = tile_w(np.concatenate(ch, 1), 256)
    wvg = tile_w(w[:, A0 + 1024 + o:A0 + 1024 + o + 256], 256)[0]
    pv = np.zeros((128, 2, 3, 16), np.float32)
    for kd, row in enumerate((b, 2)):
        m = mod3[row].reshape(6, 2048)
        for i, v in enumerate((m[0], m[1], ng[0])):
            pv[:, kd, i, :] = vecT(v)
    mu = inp["rwkv_mu"][0]
    rp = np.zeros((128, 32), np.float32)
    for i, base in enumerate((0, 1024, 2048)):
        rp[:, 2 * i:2 * i + 2] = vec2(mu[base + o:base + o + 256])
    rp[:, 6] = vecpad(mu[3072:3168]); rp[:, 7] = vecpad(mu[3168:3264])
    rp[:, 8:10] = vec2(mu[3264:3520])
    own = slice(o, o + 256)
    rp[:, 10:12] = vec2(inp["rwkv_a0"][0][own])
    for d in range(2):
        rp[:, 12 + 2 * d:14 + 2 * d] = vec2(inp["rwkv_w0"][0][d][own])
    rp[:, 16:18] = vec2(inp["rwkv_kk"][0][own])
    rp[:, 18:20] = vec2(inp["rwkv_ka"][0][own])
    rp[:, 20:22] = vec2(inp["rwkv_rk"][0].reshape(-1)[own])
    rp[:, 22:24] = vec2(inp["rwkv_ln_w"][0][own])
    rp[:, 24:26] = vec2(inp["rwkv_ln_b"][0][own])
    rp[:, 26:28] = vec2(inp["gla_norm_g"][0])
    for d in range(2):
        rp[:, 28 + d] = inp["gla_gk_b"][0][d][128 * g:128 * g + 128]
    a2 = np.ascontiguousarray(inp["rwkv_a2"][0][:, own])
    w2 = np.ascontiguousarray(np.stack([inp["rwkv_w2"][0][d][:, own] for d in range(2)], 1))
    g2 = np.ascontiguousarray(inp["rwkv_g2"][0][:, own].reshape(2, 128, 256).transpose(1, 0, 2))
    gku = np.ascontiguousarray(np.stack([inp["gla_gk_up"][0][d][:, 128 * g:128 * g + 128] for d in range(2)], 1))
    d_ = {"xT": xT_b, "wA": wA, "wvg": wvg, "pv": pv, "rp": rp, "a2": a2, "w2": w2, "g2": g2, "gku": gku}
    d_.update(consts)
    return d_


def _mods_inputs(inp, core):
    cs = np.stack([inp["c"][0], inp["c"][1], inp["c_ctx"]], 0)
    cT = np.ascontiguousarray(cs.reshape(3, 16, 128).transpose(2, 1, 0))
    cols = slice(core * 1536, (core + 1) * 1536)
    mw = np.stack([tile_w(inp["mod_w"][l][:, cols], 1536)[0] for l in range(2)], 0)
    mb = np.stack([vecT(inp["mod_b"][l][cols]) for l in range(2)], 1)
    return {"cT": cT, "mw": mw, "mb": np.ascontiguousarray(mb)}


def _run(nc, in_maps):
    res = run_bass_kernel_spmd(nc, in_maps, core_ids=list(range(8)))
    return res.results


def kernel(**inp):
    inp = {k: np.asarray(v) for k, v in inp.items()}
    r = _run(build_mods(), [_mods_inputs(inp, c) for c in range(8)])
    mod3 = np.zeros((2, 3, 12288), np.float32)
    for c in range(8):
        mo = r[c]["mo"]
        for l in range(2):
            mod3[l][:, c * 1536:(c + 1) * 1536] = mo[:, l].transpose(2, 1, 0).reshape(3, 1536)
    xT = [np.ascontiguousarray(np.concatenate([inp["ctx"][b], inp["x"][b]], 0).T) for b in range(2)]
    cst = rg_consts()
    r = _run(build_rg(), [rg_inputs(xT[c // 4], inp, mod3[0], inp["norm_g"][0], c // 4, c % 4, cst) for c in range(8)])
    m0 = [np.zeros((2048, 4352), BF) for _ in range(2)]
    for c in range(8):
        b, g = c // 4, c % 4
        mo = r[c]["moT"]
        m0[b][256 * g:256 * g + 256] = mo[0:256]
        m0[b][1024 + 256 * g:1024 + 256 * g + 256] = mo[256:512]
    def cols0(q):
        return np.concatenate([np.arange(64 * q, 64 * q + 64), 256 + np.arange(1024 * q, 1024 * q + 1024)])
    wo = tile_w(inp["ab_w_out"][0], 512)
    wi = tile_wi(inp["ffn_w_in"][0])
    w2 = tile_w(inp["ffn_w_out"][0], 128)
    ims = []
    for c in range(8):
        b, q = c // 4, c % 4
        cc = cols0(q)
        ims.append({"mT": np.ascontiguousarray(m0[b][:, cc]), "xT": np.ascontiguousarray(xT[b][:, cc]), "wo": wo, "wi": wi,
                    "w2": w2, "pv": post_pv(mod3[0], inp["norm_g"][0], b)})
    r = _run(build_post(16, [(64, 1), (512, 0), (512, 0)], 512), ims)
    h1T = [np.zeros((2048, 4352), np.float32) for _ in range(2)]
    for c in range(8):
        b, q = c // 4, c % 4
        h1T[b][:, cols0(q)] = r[c]["oT"]
    cs = rot_tables()
    r = _run(build_ret(), [ret_inputs(h1T[c // 4], inp["ret_w_in"][0], mod3[1], inp["norm_g"][1], c // 4, c % 4, cs) for c in range(8)])
    m1 = [np.zeros((4096, 4096), BF) for _ in range(2)]
    for c in range(8):
        b, hp = c // 4, c % 4
        m1[b][1024 * hp:1024 * hp + 1024] = r[c]["moT"]
    wo = tile_w(inp["ret_w_out"][0], 128)
    wi = tile_wi(inp["ffn_w_in"][1])
    w2 = tile_w(inp["ffn_w_out"][1], 128)
    ims = []
    for c in range(8):
        b, q = c // 4, c % 4
        ims.append({"mT": np.ascontiguousarray(m1[b][:, 1024 * q:1024 * q + 1024]),
                    "xT": np.ascontiguousarray(h1T[b][:, 256 + 1024 * q:256 + 1024 * q + 1024]), "wo": wo, "wi": wi,
                    "w2": w2, "pv": post_pv(mod3[1], inp["norm_g"][1], b)})
    r = _run(build_post(32, [(512, 0), (512, 0)], 128), ims)
    out = np.zeros((2, 4096, 2048), np.float32)
    for c in range(8):
        b, q = c // 4, c % 4
        out[b, 1024 * q:1024 * q + 1024, :] = r[c]["oT"].T
    return out
```

```python
import numpy as np
import ml_dtypes
from contextlib import ExitStack
import concourse.bass as bass
import concourse.mybir as mybir
from concourse.bass_utils import run_bass_kernel_spmd


F32 = mybir.dt.float32
BF16 = mybir.dt.bfloat16
AF = mybir.ActivationFunctionType
ALU = mybir.AluOpType
AX = mybir.AxisListType

ENGS = ("pe", "act", "dve", "pool", "sp")
N_DMA_SEMS = 40


class Tl:
    __slots__ = ("ap", "name", "lw", "rd", "pw")

    def __init__(self, ap, name=""):
        self.ap = ap
        self.name = name
        self.lw = None
        self.rd = []
        self.pw = []

    def __getitem__(self, idx):
        return Vw(self, self.ap[idx])

    def v(self):
        return Vw(self, self.ap)

    def sub(self, idx, name=""):
        return Tl(self.ap[idx], name or self.name)


class Vw:
    __slots__ = ("t", "ap")

    def __init__(self, t, ap):
        self.t = t
        self.ap = ap

    def __getitem__(self, idx):
        return Vw(self.t, self.ap[idx])

    def bitcast(self, dt):
        return Vw(self.t, self.ap.bitcast(dt))

    def rearrange(self, *a, **k):
        return Vw(self.t, self.ap.rearrange(*a, **k))

    def with_ap(self, ap):
        return Vw(self.t, ap)


class Op:
    __slots__ = ("eng", "fn", "deps", "is_dma", "tok", "sig", "idx", "epoch", "force")


class Prog:
    def __init__(self, nc, arena_words=52736, same_engine_sync=True):
        self.nc = nc
        self.ops = []
        self.same_engine_sync = same_engine_sync
        self.arena_words = arena_words
        self.arena = None
        self.off = 0
        self.scopes = []
        self.dma_since_barrier = []
        self.last_on_eng = {e: None for e in ENGS}
        self.last_comp = {e: None for e in ENGS}
        self.psum = []
        self.ndma = 0
        self.high = 0
        self.dma_ops = []
        self.epoch = 0

    def setup_mem(self, stack):
        nc = self.nc
        self.arena = stack.enter_context(nc.sbuf_tensor("arena", [128, self.arena_words], F32))
        for i in range(8):
            t = stack.enter_context(nc.psum_tensor(f"psb{i}", [128, 512], F32))
            self.psum.append(Tl(t[:, :], f"ps{i}"))
        self.ps_rr = 0

    def ps(self):
        t = self.psum[self.ps_rr % 8]
        self.ps_rr += 1
        return t

    def alloc(self, shape, dtype=F32, name=""):
        p = shape[0]
        free = int(np.prod(shape[1:]))
        isz = 2 if dtype == BF16 else 4
        nw = (free * isz + 3) // 4
        nw = (nw + 7) // 8 * 8
        if self.off + nw > self.arena_words:
            raise RuntimeError(f"arena overflow allocating {name} {shape}: off={self.off} nw={nw}")
        ap = self.arena[0:p, self.off:self.off + nw]
        self.off += nw
        self.high = max(self.high, self.off)
        if dtype != F32:
            ap = ap.bitcast(dtype)
        ap = ap[:, 0:free]
        if len(shape) == 3:
            ap = ap.rearrange("p (a b) -> p a b", a=shape[1])
        elif len(shape) == 4:
            ap = ap.rearrange("p (a b c) -> p a b c", a=shape[1], b=shape[2])
        return Tl(ap, name)

    def alloc_at(self, byte_off, shape, dtype=F32, name=""):
        p = shape[0]
        free = int(np.prod(shape[1:]))
        isz = 2 if dtype == BF16 else 4
        nw = (free * isz + 3) // 4
        assert byte_off % 32 == 0
        w0 = byte_off // 4
        if w0 + nw > self.arena_words:
            raise RuntimeError(f"arena overflow allocating {name} {shape} at {byte_off}")
        ap = self.arena[0:p, w0:w0 + nw]
        if dtype != F32:
            ap = ap.bitcast(dtype)
        ap = ap[:, 0:free]
        if len(shape) == 3:
            ap = ap.rearrange("p (a b) -> p a b", a=shape[1])
        elif len(shape) == 4:
            ap = ap.rearrange("p (a b c) -> p a b c", a=shape[1], b=shape[2])
        return Tl(ap, name)

    def push(self):
        self.scopes.append(self.off)

    def pop(self):
        self.barrier()
        self.off = self.scopes.pop()

    def _add(self, eng, fn, reads, writes, is_dma=False, pwrites=()):
        op = Op()
        op.eng = eng
        op.fn = fn
        op.is_dma = is_dma
        op.idx = len(self.ops)
        op.sig = False
        op.tok = None
        op.epoch = self.epoch
        op.force = False
        deps = set()
        for v in reads:
            t = v.t if isinstance(v, Vw) else v
            if t.lw is not None:
                deps.add(t.lw)
            deps.update(t.pw)
        for v in writes:
            t = v.t if isinstance(v, Vw) else v
            if t.lw is not None:
                deps.add(t.lw)
            deps.update(t.rd)
            deps.update(t.pw)
        for v in pwrites:
            t = v.t if isinstance(v, Vw) else v
            if t.lw is not None:
                deps.add(t.lw)
            deps.update(t.rd)
        if len(deps) > 1:
            latest = {}
            keep = set()
            for d_ in deps:
                o_ = self.ops[d_]
                if o_.is_dma:
                    keep.add(d_)
                elif latest.get(o_.eng, -1) < d_:
                    latest[o_.eng] = d_
            keep.update(latest.values())
            deps = keep
        if is_dma:
            k = self.ndma
            self.ndma += 1
            op.tok = ("dma", k % N_DMA_SEMS, 16 * (k // N_DMA_SEMS + 1))
            if k >= N_DMA_SEMS:
                deps.add(self.dma_ops[k - N_DMA_SEMS])
            self.dma_ops.append(op.idx)
            self.dma_since_barrier.append(op.idx)
        op.deps = deps
        self.ops.append(op)
        for v in reads:
            t = v.t if isinstance(v, Vw) else v
            t.rd.append(op.idx)
        for v in writes:
            t = v.t if isinstance(v, Vw) else v
            t.lw = op.idx
            t.rd = []
            t.pw = []
        for v in pwrites:
            t = v.t if isinstance(v, Vw) else v
            t.pw.append(op.idx)
        self.last_on_eng[eng] = op.idx
        if not is_dma:
            self.last_comp[eng] = op.idx
        return op

    def barrier(self):
        dmas = list(self.dma_since_barrier)
        self.dma_since_barrier = []
        arr = []
        prev = dict(self.last_comp)
        for e in ENGS:
            op = self._add(e, lambda eh: eh.nop(), [], [])
            if prev[e] is not None:
                op.deps.add(prev[e])
            op.deps.update(dmas)
            op.force = True
            arr.append(op.idx)
        for e in ENGS:
            op = self._add(e, lambda eh: eh.nop(), [], [])
            op.deps.update(arr)
            op.force = True
        self.epoch += 1

    @staticmethod
    def _ap(x):
        return x.ap if isinstance(x, Vw) else x

    def matmul(self, out, lhsT, rhs, start=True, stop=True, **kw):
        o, l, r = out.ap, lhsT.ap, rhs.ap
        return self._add("pe", lambda e: e.matmul(o, l, r, start=start, stop=stop, **kw),
                         [lhsT, rhs], [out])

    def transpose(self, out, in_, ident):
        o, i, d = out.ap, in_.ap, ident.ap
        return self._add("pe", lambda e: e.transpose(o, i, d), [in_, ident], [out])

    def act(self, out, in_, func, bias=None, scale=None, accum_out=None, eng="act"):
        kw = {}
        reads = [in_]
        writes = [out]
        if bias is not None:
            if isinstance(bias, Vw):
                reads.append(bias)
                kw["bias"] = bias.ap
            else:
                kw["bias"] = bias
        if scale is not None:
            if isinstance(scale, Vw):
                reads.append(scale)
                kw["scale"] = scale.ap
            else:
                kw["scale"] = scale
        if accum_out is not None:
            writes.append(accum_out)
            kw["accum_out"] = accum_out.ap
        o, i = out.ap, in_.ap
        return self._add(eng, lambda e: e.activation(o, i, func, **kw), reads, writes)

    def tt(self, out, in0, in1, op, eng="dve"):
        o, a, b = out.ap, in0.ap, in1.ap
        return self._add(eng, lambda e: e.tensor_tensor(o, a, b, op), [in0, in1], [out])

    def ts(self, out, in0, s1, op0, s2=None, op1=None, eng="dve", accum_out=None):
        reads = [in0]
        writes = [out]
        a1 = s1
        a2 = s2
        if isinstance(s1, Vw):
            reads.append(s1)
            a1 = s1.ap
        if isinstance(s2, Vw):
            reads.append(s2)
            a2 = s2.ap
        o, i = out.ap, in0.ap
        kw = {}
        if accum_out is not None:
            writes.append(accum_out)
            kw["accum_out"] = accum_out.ap
        if op1 is None:
            return self._add(eng, lambda e: e.tensor_scalar(o, i, a1, None, op0, **kw), reads, writes)
        return self._add(eng, lambda e: e.tensor_scalar(o, i, a1, a2, op0, op1, **kw), reads, writes)

    def stt(self, out, in0, scalar, in1, op0, op1, eng="dve"):
        reads = [in0, in1]
        s = scalar
        if isinstance(scalar, Vw):
            reads.append(scalar)
            s = scalar.ap
        o, a, b = out.ap, in0.ap, in1.ap
        return self._add(eng, lambda e: e.scalar_tensor_tensor(o, a, s, b, op0, op1), reads, [out])

    def copy(self, out, in_, eng="dve"):
        o, i = out.ap, in_.ap
        if eng == "act":
            return self._add(eng, lambda e: e.copy(o, i), [in_], [out])
        return self._add(eng, lambda e: e.tensor_copy(o, i), [in_], [out])

    def memset(self, out, val, eng="dve"):
        o = out.ap
        return self._add(eng, lambda e: e.memset(o, val), [], [out])

    def scan(self, out, d0, d1, initial, op0, op1):
        reads = [d0, d1]
        ini = initial
        if isinstance(initial, Vw):
            reads.append(initial)
            ini = initial.ap
        o, a, b = out.ap, d0.ap, d1.ap
        return self._add("dve", lambda e: e.tensor_tensor_scan(o, a, b, ini, op0, op1), reads, [out])

    def recip(self, out, in_):
        o, i = out.ap, in_.ap
        return self._add("dve", lambda e: e.reciprocal(o, i), [in_], [out])

    def dma(self, out, in_, q="sp", reads=None, writes=None, pwrites=()):
        o, i = self._ap(out), self._ap(in_)
        rd = [in_] if isinstance(in_, Vw) else []
        wr = [out] if isinstance(out, Vw) else []
        if reads:
            rd += reads
        if writes:
            wr += writes
        return self._add(q, lambda e: e.dma_start(out=o, in_=i), rd, wr, is_dma=True, pwrites=pwrites)

    def collective(self, kind, out_t, in_t, groups, op=None):
        o, i = out_t.ap, in_t.ap
        aop = ALU.bypass if op is None else op
        return self._add("pool", lambda e: e.collective_compute(kind, aop, groups, [i], [o]),
                         [in_t], [out_t], is_dma=True)

    def raw(self, eng, fn, reads, writes):
        return self._add(eng, fn, reads, writes)

    def emit(self, stack, final_wait_ops=()):
        nc = self.nc
        ops = self.ops
        NSET = 3

        def live(op, dop):
            if dop.is_dma:
                return True
            if dop.epoch != op.epoch:
                return False
            if op.force:
                return True
            if dop.eng == "pe" and op.eng == "pe":
                return False
            if (not self.same_engine_sync) and dop.eng == op.eng:
                return False
            return True
        for op in ops:
            for d in op.deps:
                dop = ops[d]
                if (not dop.is_dma) and live(op, dop):
                    dop.sig = True
        for i in final_wait_ops:
            ops[i].sig = True
        esem = [{e: stack.enter_context(nc.semaphore(f"s_{e}{k}")) for e in ENGS} for k in range(NSET)]
        dsem = [stack.enter_context(nc.semaphore(f"s_dma{i}")) for i in range(N_DMA_SEMS)]
        cnt = {}
        maxcnt = 0
        for op in ops:
            if op.is_dma:
                op.tok = (dsem[op.tok[1]], op.tok[2])
            elif op.sig:
                key = (op.eng, op.epoch)
                cnt[key] = cnt.get(key, 0) + 1
                maxcnt = max(maxcnt, cnt[key])
                op.tok = (esem[op.epoch % NSET][op.eng], cnt[key])
        self.sig_counts = maxcnt
        per = {e: [] for e in ENGS}
        for op in ops:
            per[op.eng].append(op)
        block = stack.enter_context(nc.Block())
        nwaits = {e: 0 for e in ENGS}
        dsem_ids = set(id(x) for x in dsem)

        def body(ename, extra_final):
            def run(eh):
                waited = {}
                cur_epoch = 0
                for op in per[ename]:
                    if op.epoch != cur_epoch:
                        for ep in range(cur_epoch + 1, op.epoch + 1):
                            if ep + 1 >= NSET:
                                eh.sem_clear(esem[(ep + 1) % NSET][ename])
                        cur_epoch = op.epoch
                        waited = {k: v for k, v in waited.items() if k in dsem_ids}
                    need = {}
                    for d in op.deps:
                        dop = ops[d]
                        if not live(op, dop):
                            continue
                        sem, val = dop.tok
                        key = id(sem)
                        if waited.get(key, 0) >= val:
                            continue
                        if key not in need or need[key][1] < val:
                            need[key] = (sem, val)
                    for key, (sem, val) in need.items():
                        eh.wait_ge(sem, val)
                        waited[key] = val
                        nwaits[ename] += 1
                    ins = op.fn(eh)
                    if op.is_dma:
                        ins.then_inc(op.tok[0], 16)
                    elif op.sig:
                        ins.then_inc(op.tok[0], 1)
                if extra_final:
                    for i in final_wait_ops:
                        sem, val = ops[i].tok
                        eh.wait_ge(sem, val)
            return run

        block.tensor(body("pe", False))
        block.scalar(body("act", False))
        block.vector(body("dve", False))
        block.gpsimd(body("pool", False))
        block.sync(body("sp", True))
        self.nwaits = nwaits


D = 2048
KC_D = 16
DFF = 5632
RMS_EPS = 1e-6


def bc_mid(v, reps):
    ap = v.ap
    (ps, pn), (s1, n1) = ap.ap
    return Vw(v.t, bass.AP(ap.tensor, ap.offset, [[ps, pn], [0, reps], [s1, n1]]))


def bc_inner(v, reps):
    ap = v.ap
    (ps, pn), (s1, n1) = ap.ap
    return Vw(v.t, bass.AP(ap.tensor, ap.offset, [[ps, pn], [s1, n1], [0, reps]]))


def make_consts(P):
    c = {}
    ones = P.alloc([128, 128], BF16, "ones_bf")
    P.memset(ones.v(), 1.0)
    c["ones_bf"] = ones
    idf = P.alloc([128, 128], F32, "ident_f")
    P.memset(idf.v(), 1.0, eng="pool")
    ia = idf.ap
    P.raw("pool", lambda e: e.affine_select(ia, ia, [[-1, 128]], ALU.is_equal, 0.0, base=0,
                                            channel_multiplier=1), [idf.v()], [idf.v()])
    c["ident_f"] = idf
    idb = P.alloc([128, 128], BF16, "ident_b")
    P.copy(idb.v(), idf.v())
    c["ident_b"] = idb
    return c


def norm_stats(P, C, X, KC, n, sq, rstd, dim, eps=RMS_EPS, rs_eng="dve"):
    P.act(sq[:, 0:KC, 0:n], X, AF.Square)
    ps = P.ps()
    for c in range(KC):
        P.matmul(ps[:, 0:n], C["ones_bf"].v(), sq[:, c, 0:n], start=(c == 0), stop=(c == KC - 1))
    P.act(rstd[:, 0:n], ps[:, 0:n], AF.Sqrt, scale=1.0 / dim, bias=float(eps))
    P.recip(rstd[:, 0:n], rstd[:, 0:n])


def proj(P, Wt, groups, KC, rhs_tiles, epilogue, wbufs, q="pool", pair=None):
    ncg = Wt.shape[3]
    for gi, g in enumerate(groups):
        wb = wbufs[gi % len(wbufs)]
        P.dma(wb[:, 0:KC, 0:ncg], Wt[g], q=q)
        for mi in range((ncg + 127) // 128):
            msz = min(128, ncg - mi * 128)
            for ti, (rhs, n) in enumerate(rhs_tiles):
                ps = P.ps()
                for k in range(KC):
                    P.matmul(ps[0:msz, 0:n], wb[:, k, mi * 128:mi * 128 + msz], rhs[:, k, 0:n],
                             start=(k == 0), stop=(k == KC - 1))
                epilogue(gi, mi, ti, ps, msz, n)


def load_pv(P, pvt, modD, ng, layer, mod_ids, ng_ids):
    for kd in range(2):
        for i, v in enumerate(mod_ids):
            P.dma(pvt[:, kd, i, :], modD[:, layer, kd, v * 16:(v + 1) * 16])
        for j, gi in enumerate(ng_ids):
            P.dma(pvt[:, kd, len(mod_ids) + j, :], ng[:, layer, gi, :])


def build_mods():
    nc = bass.Bass("TRN2", target_bir_lowering=False)
    cT = nc.dram_tensor("cT", [128, 16, 3], F32, kind="ExternalInput").ap()
    mw = nc.dram_tensor("mw", [2, 128, 16, 1536], F32, kind="ExternalInput").ap()
    mb = nc.dram_tensor("mb", [128, 2, 12], F32, kind="ExternalInput").ap()
    mo = nc.dram_tensor("mo", [128, 2, 12, 3], F32, kind="ExternalOutput").ap()
    P = Prog(nc)
    with ExitStack() as st:
        P.setup_mem(st)
        outs = emit_mods(P, cT, mw, mb, mo)
        P.emit(st, final_wait_ops=outs)
    return nc


def emit_mods(P, cT, mw, mb, mo):
    ct = P.alloc([128, 16, 3], F32, "ct")
    sc = P.alloc([128, 16, 3], F32, "sc")
    mbt = P.alloc([128, 2, 12], F32, "mbt")
    res = P.alloc([128, 2, 12, 3], F32, "res")
    w = P.alloc([128, 16, 1536], F32, "mw")
    P.dma(ct.v(), cT)
    P.dma(mbt.v(), mb)
    P.act(sc.v(), ct.v(), AF.Silu)
    for l in range(2):
        for h in range(2):
            P.dma(w[:, h * 8:(h + 1) * 8, :], mw[l, :, h * 8:(h + 1) * 8, :])
        for m in range(12):
            ps = P.ps()
            for k in range(16):
                P.matmul(ps[:, 0:3], w[:, k, m * 128:(m + 1) * 128], sc[:, k, :], start=(k == 0), stop=(k == 15))
            P.ts(res[:, l, m, :], ps[:, 0:3], mbt[:, l, m:m + 1], ALU.add)
    return [P.dma(mo, res.v()).idx]


def build_post(KCm, tiles, wout_g):
    NT = sum(n for n, _ in tiles)
    nc = bass.Bass("TRN2", target_bir_lowering=False)
    mT = nc.dram_tensor("mT", [KCm * 128, NT], BF16, kind="ExternalInput").ap()
    xT = nc.dram_tensor("xT", [D, NT], F32, kind="ExternalInput").ap()
    Gwo = D // wout_g
    wo = nc.dram_tensor("wo", [Gwo, 128, KCm, wout_g], F32, kind="ExternalInput").ap()
    wi = nc.dram_tensor("wi", [22, 128, 16, 512], F32, kind="ExternalInput").ap()
    w2 = nc.dram_tensor("w2", [16, 128, 44, 128], F32, kind="ExternalInput").ap()
    pv = nc.dram_tensor("pv", [128, 2, 7, 16], F32, kind="ExternalInput").ap()
    oT = nc.dram_tensor("oT", [D, NT], F32, kind="ExternalOutput").ap()
    hsp = Tl(nc.dram_tensor("hsp", [D, NT], F32, kind="Internal").ap(), "hsp")
    P = Prog(nc)
    with ExitStack() as st:
        P.setup_mem(st)
        emit_post(P, KCm, tiles, wout_g, mT, xT, wo, wi, w2, pv, oT, hsp, st)
    return nc


def emit_post(P, KCm, tiles, wout_g, mT, xT, wo, wi, w2, pv, oT, hsp, st, final=True):
    NT = sum(n for n, _ in tiles)
    Gwo = D // wout_g
    KB = 1024
    C = make_consts(P)
    pvt = P.alloc([128, 2, 7, 16], F32, "pv")
    P.dma(pvt.v(), pv)
    der = P.alloc([128, 2, 3, 16], F32, "der")
    for kd in range(2):
        P.tt(der[:, kd, 0, :], pvt[:, kd, 0, :], pvt[:, kd, 4, :], ALU.mult)
        P.ts(der[:, kd, 1, :], pvt[:, kd, 2, :], 1.0, ALU.add)
        P.tt(der[:, kd, 1, :], der[:, kd, 1, :], pvt[:, kd, 5, :], ALU.mult)
        P.tt(der[:, kd, 2, :], pvt[:, kd, 3, :], pvt[:, kd, 6, :], ALU.mult)
    sq = P.alloc([128, 16, 256], BF16, "sq")
    rstd = P.alloc([128, 256], F32, "rstd")
    assert P.off * 4 <= 12 * KB, P.off * 4
    subt = []
    ntl = []
    t0 = 0
    for n, kd in tiles:
        o = 0
        while o < n:
            m = min(256, n - o)
            subt.append((t0 + o, m, kd))
            o += m
        ntl.append((t0, n))
        t0 += n
    A2 = 12 * KB
    RB = 47 * KB
    RA = 117 * KB
    a2 = P.alloc_at(A2, [128, 16, NT], BF16, "a2")
    yT = P.alloc_at(RB, [128, 16, NT], F32, "yT")
    mt = P.alloc_at(RA, [128, KCm, NT], BF16, "mt")
    wsz = KCm * wout_g * 2
    assert KCm * NT * 2 + 2 * wsz <= 83 * KB
    wb = [P.alloc_at(RA + KCm * NT * 2 + i * wsz, [128, KCm, wout_g], BF16, f"wo{i}") for i in range(2)]
    P.dma(mt.v(), mT.rearrange("(c p) n -> p c n", p=128))
    flip = [0]

    def ep1(gi, mi, ti, ps, msz, n):
        ch = gi * (wout_g // 128) + mi
        a, nn = ntl[ti]
        flip[0] ^= 1
        P.copy(yT[:, ch, a:a + nn], ps[:, 0:nn], eng="act" if flip[0] else "dve")
    proj(P, wo, list(range(Gwo)), KCm, [(mt[:, :, a:a + n], n) for a, n in ntl], ep1, wb)
    P.barrier()
    xt = [P.alloc_at(RA + i * 16 * KB, [128, 16, 256], F32, f"xt{i}") for i in range(2)]
    tmp = P.alloc_at(RA + 32 * KB, [128, 16, 256], F32, "tmp")
    for si, (a, n, kd) in enumerate(subt):
        x_ = xt[si % 2]
        norm_stats(P, C, yT[:, :, a:a + n], 16, n, sq, rstd, D)
        P.dma(x_[:, :, 0:n], xT[:, a:a + n].rearrange("(c p) n -> p c n", p=128))
        P.tt(yT[:, :, a:a + n], yT[:, :, a:a + n], bc_mid(rstd[:, 0:n], 16), ALU.mult)
        for c in range(16):
            P.stt(yT[:, c, a:a + n], yT[:, c, a:a + n], der[:, kd, 0, c:c + 1], x_[:, c, 0:n], ALU.mult, ALU.add)
        P.dma(hsp[:, a:a + n].rearrange("(c p) n -> p c n", p=128), yT[:, :, a:a + n])
        norm_stats(P, C, yT[:, :, a:a + n], 16, n, sq, rstd, D)
        P.tt(tmp[:, :, 0:n], yT[:, :, a:a + n], bc_mid(rstd[:, 0:n], 16), ALU.mult)
        for c in range(16):
            P.act(a2[:, c, a:a + n], tmp[:, c, 0:n], AF.Identity, scale=der[:, kd, 1, c:c + 1], bias=pvt[:, kd, 1, c:c + 1])
    P.barrier()
    hm = P.alloc_at(RB, [128, 44, NT], BF16, "hm")
    WB0 = RB + 96 * KB
    wib = [P.alloc_at(WB0 + i * 16 * KB, [128, 16, 512], BF16, f"wi{i}") for i in range(2)]
    sgt = [P.alloc_at(WB0 + 32 * KB + i * 2 * KB, [128, 512], F32, f"sg{i}") for i in range(4)]
    sgi = [0]
    for g in range(22):
        wbuf = wib[g % 2]
        P.dma(wbuf.v(), wi[g], q="pool")
        for mi in range(2):
            ch = g * 2 + mi
            for (a, n) in ntl:
                psg = P.ps()
                for k in range(16):
                    P.matmul(psg[:, 0:n], wbuf[:, k, mi * 128:(mi + 1) * 128], a2[:, k, a:a + n], start=(k == 0), stop=(k == 15))
                psu = P.ps()
                for k in range(16):
                    P.matmul(psu[:, 0:n], wbuf[:, k, 256 + mi * 128:256 + (mi + 1) * 128], a2[:, k, a:a + n], start=(k == 0), stop=(k == 15))
                s_ = sgt[sgi[0] % 4]
                sgi[0] += 1
                P.act(s_[:, 0:n], psg[:, 0:n], AF.Silu)
                P.tt(hm[:, ch, a:a + n], s_[:, 0:n], psu[:, 0:n], ALU.mult)
    P.barrier()
    fa = P.alloc_at(A2, [128, 8, NT], F32, "fa")
    fb = P.alloc_at(WB0, [128, 8, NT], F32, "fb")
    w2b = [P.alloc_at(178 * KB + i * 11 * KB, [128, 44, 128], BF16, f"w2{i}") for i in range(2)]

    def ep3(gi, mi, ti, ps, msz, n):
        a, nn = ntl[ti]
        flip[0] ^= 1
        dst = fa if gi < 8 else fb
        P.copy(dst[:, gi % 8, a:a + nn], ps[:, 0:nn], eng="act" if flip[0] else "dve")
    proj(P, w2, list(range(16)), 44, [(hm[:, :, a:a + n], n) for a, n in ntl], ep3, w2b)
    P.barrier()
    h1t = [P.alloc_at(RB + i * 16 * KB, [128, 16, 256], F32, f"h1t{i}") for i in range(2)]
    outs = []
    for si, (a, n, kd) in enumerate(subt):
        h_ = h1t[si % 2]
        P.dma(h_[:, :, 0:n], hsp[:, a:a + n].rearrange("(c p) n -> p c n", p=128))
        P.act(sq[:, 0:8, 0:n], fa[:, :, a:a + n], AF.Square)
        P.act(sq[:, 8:16, 0:n], fb[:, :, a:a + n], AF.Square)
        ps = P.ps()
        for c in range(16):
            P.matmul(ps[:, 0:n], C["ones_bf"].v(), sq[:, c, 0:n], start=(c == 0), stop=(c == 15))
        P.act(rstd[:, 0:n], ps[:, 0:n], AF.Sqrt, scale=1.0 / D, bias=float(RMS_EPS))
        P.recip(rstd[:, 0:n], rstd[:, 0:n])
        P.tt(fa[:, :, a:a + n], fa[:, :, a:a + n], bc_mid(rstd[:, 0:n], 8), ALU.mult)
        P.tt(fb[:, :, a:a + n], fb[:, :, a:a + n], bc_mid(rstd[:, 0:n], 8), ALU.mult)
        for c in range(16):
            src = fa if c < 8 else fb
            P.stt(h_[:, c, 0:n], src[:, c % 8, a:a + n], der[:, kd, 2, c:c + 1], h_[:, c, 0:n], ALU.mult, ALU.add)
        outs.append(P.dma(oT[:, a:a + n].rearrange("(c p) n -> p c n", p=128), h_[:, :, 0:n]).idx)
    if final:
        P.emit(st, final_wait_ops=outs)
    return outs


RC = 128
T_CTX = 256
T_LAT = 4096
T_ALL = T_CTX + T_LAT


def build_ret():
    nc = bass.Bass("TRN2", target_bir_lowering=False)
    hT = nc.dram_tensor("hT", [D, T_ALL], F32, kind="ExternalInput").ap()
    wqkg = nc.dram_tensor("wqkg", [8, 128, 16, 256], F32, kind="ExternalInput").ap()
    wv = nc.dram_tensor("wv", [2, 128, 16, 512], F32, kind="ExternalInput").ap()
    pv = nc.dram_tensor("pv", [128, 2, 3, 16], F32, kind="ExternalInput").ap()
    cs = nc.dram_tensor("cs", [128, 2, T_LAT], F32, kind="ExternalInput").ap()
    mk = nc.dram_tensor("mk", [128, 2, 2, 128], F32, kind="ExternalInput").ap()
    dr = nc.dram_tensor("dr", [128, 2, 2, 128], F32, kind="ExternalInput").ap()
    kdc = nc.dram_tensor("kdc", [128, 2, 2], F32, kind="ExternalInput").ap()
    gC = nc.dram_tensor("gC", [128, 2], F32, kind="ExternalInput").ap()
    moT = nc.dram_tensor("moT", [1024, T_LAT], BF16, kind="ExternalOutput").ap()
    P = Prog(nc)
    with ExitStack() as st:
        P.setup_mem(st)
        outs = emit_ret(P, nc, hT, wqkg, wv, pv, cs, mk, dr, kdc, gC, moT)
        P.emit(st, final_wait_ops=outs)
    return nc


def emit_ret(P, nc, hT, wqkg, wv, pv, cs, mk, dr, kdc, gC, moT, sfx=""):
    KB = 1024
    qTs = nc.dram_tensor("qTs" + sfx, [128, 4, T_ALL], BF16, kind="Internal").ap()
    kTs = nc.dram_tensor("kTs" + sfx, [128, 4, T_ALL], BF16, kind="Internal").ap()
    Vs = nc.dram_tensor("Vs" + sfx, [T_ALL, 1024], BF16, kind="Internal").ap()
    sgs = nc.dram_tensor("sgs" + sfx, [128, 8, T_LAT], BF16, kind="Internal").ap()
    ofs = nc.dram_tensor("ofs" + sfx, [128, 8, T_LAT], F32, kind="Internal").ap()
    NCH = T_ALL // RC
    qT_c = [Tl(qTs[:, :, c * RC:(c + 1) * RC], f"qTs{c}") for c in range(NCH)]
    kT_c = [Tl(kTs[:, :, c * RC:(c + 1) * RC], f"kTs{c}") for c in range(NCH)]
    V_c = [Tl(Vs[c * RC:(c + 1) * RC, :], f"Vs{c}") for c in range(NCH)]
    sg_c = [Tl(sgs[:, :, c * RC:(c + 1) * RC], f"sg{c}") for c in range(T_LAT // RC)]
    of_c = [Tl(ofs[:, :, c * RC:(c + 1) * RC], f"of{c}") for c in range(T_LAT // RC)]

    C = make_consts(P)
    pvt = P.alloc([128, 2, 3, 16], F32, "pv")
    P.dma(pvt.v(), pv)
    der = P.alloc([128, 2, 16], F32, "der")
    for kd in range(2):
        P.ts(der[:, kd, :], pvt[:, kd, 1, :], 1.0, ALU.add)
        P.tt(der[:, kd, :], der[:, kd, :], pvt[:, kd, 2, :], ALU.mult)
    mkt = P.alloc([128, 2, 2, 128], F32, "mk")
    P.dma(mkt.v(), mk)
    drt = P.alloc([128, 2, 2, 128], F32, "dr")
    P.dma(drt.v(), dr)
    kdt = P.alloc([128, 2, 2], F32, "kdc")
    P.dma(kdt.v(), kdc)
    gct = P.alloc([128, 2], F32, "gC")
    P.dma(gct.v(), gC)
    sq = P.alloc([128, 16, 128], BF16, "sq")
    rstd = P.alloc([128, 128], F32, "rstd")
    base = P.off * 4
    assert base <= 14 * KB, base
    aT = P.alloc_at(14 * KB, [128, 16, T_ALL], BF16, "aT")
    R2 = 153 * KB
    ht = [P.alloc_at(R2 + i * 8 * KB, [128, 16, 128], F32, f"ht{i}") for i in range(2)]
    tmp = P.alloc_at(R2 + 16 * KB, [128, 16, 128], F32, "tmp")
    for si in range(T_ALL // 128):
        a = si * 128
        kd = 1 if a < T_CTX else 0
        h_ = ht[si % 2]
        P.dma(h_.v(), hT[:, a:a + 128].rearrange("(c p) n -> p c n", p=128))
        norm_stats(P, C, h_.v(), 16, 128, sq, rstd, D)
        P.tt(tmp.v(), h_.v(), bc_mid(rstd[:, 0:128], 16), ALU.mult)
        for c in range(16):
            P.act(aT[:, c, a:a + 128], tmp[:, c, :], AF.Identity, scale=der[:, kd, c:c + 1], bias=pvt[:, kd, 0, c:c + 1])
    P.barrier()
    wb = [P.alloc_at(R2 + i * 8 * KB, [128, 16, 256], BF16, f"wb{i}") for i in range(2)]
    cst = [P.alloc_at(R2 + 16 * KB + i * 4 * KB, [128, 2, 512], F32, f"cs{i}") for i in range(2)]
    raw = [P.alloc_at(R2 + 24 * KB + i * 2 * KB, [128, 512], F32, f"raw{i}") for i in range(4)]
    tm = [P.alloc_at(R2 + 32 * KB + i * 2 * KB, [128, 512], F32, f"tm{i}") for i in range(4)]
    ob = [P.alloc_at(R2 + 40 * KB + i * 1 * KB, [128, 512], BF16, f"ob{i}") for i in range(4)]
    ttl = [(0, 256)] + [(256 + i * 512, 512) for i in range(8)]
    cnt = [0]
    for g in range(8):
        w_ = wb[g % 2]
        P.dma(w_.v(), wqkg[g], q="pool")
        for ti, (a, n) in enumerate(ttl):
            is_ctx = a < T_CTX
            if g >= 4 and is_ctx:
                continue
            pss = []
            for mi in range(2):
                ps = P.ps()
                for k in range(16):
                    P.matmul(ps[:, 0:n], w_[:, k, mi * 128:(mi + 1) * 128], aT[:, k, a:a + n], start=(k == 0), stop=(k == 15))
                pss.append(ps)
            if g < 4:
                lh = g % 2
                dst = qTs if g < 2 else kTs
                dst_c = qT_c if g < 2 else kT_c
                scl = 1.0 if g < 2 else 0.0625
                i0 = cnt[0] % 2
                cnt[0] += 1
                r0, r1 = raw[i0 * 2], raw[i0 * 2 + 1]
                o0, o1 = ob[i0 * 2], ob[i0 * 2 + 1]
                wr = [dst_c[c].v() for c in range(a // RC, (a + n) // RC)]
                if is_ctx:
                    P.act(o0[:, 0:n], pss[0][:, 0:n], AF.Identity, scale=scl)
                    P.act(o1[:, 0:n], pss[1][:, 0:n], AF.Identity, scale=scl)
                else:
                    P.act(r0[:, 0:n], pss[0][:, 0:n], AF.Identity, scale=scl)
                    P.act(r1[:, 0:n], pss[1][:, 0:n], AF.Identity, scale=scl)
                    c_ = cst[ti % 2]
                    if g == 0 or True:
                        P.dma(c_[:, :, 0:n], cs[:, :, a - T_CTX:a - T_CTX + n])
                    t0_, t1_, t2_, t3_ = tm[0], tm[1], tm[2], tm[3]
                    P.tt(t0_[:, 0:n], r0[:, 0:n], c_[:, 0, 0:n], ALU.mult)
                    P.tt(t1_[:, 0:n], r1[:, 0:n], c_[:, 1, 0:n], ALU.mult)
                    P.tt(o0[:, 0:n], t0_[:, 0:n], t1_[:, 0:n], ALU.subtract)
                    P.tt(t2_[:, 0:n], r0[:, 0:n], c_[:, 1, 0:n], ALU.mult, eng="pool")
                    P.tt(t3_[:, 0:n], r1[:, 0:n], c_[:, 0, 0:n], ALU.mult, eng="pool")
                    P.tt(o1[:, 0:n], t2_[:, 0:n], t3_[:, 0:n], ALU.add, eng="pool")
                P.dma(dst[:, lh * 2 + 0, a:a + n], o0[:, 0:n], pwrites=wr)
                P.dma(dst[:, lh * 2 + 1, a:a + n], o1[:, 0:n], pwrites=wr)
            else:
                lh = (g - 4) // 2
                vc0 = ((g - 4) % 2) * 2
                i0 = cnt[0] % 2
                cnt[0] += 1
                al = a - T_CTX
                wr = [sg_c[c].v() for c in range(al // RC, (al + n) // RC)]
                for mi in range(2):
                    o_ = ob[i0 * 2 + mi]
                    P.act(o_[:, 0:n], pss[mi][:, 0:n], AF.Silu)
                    P.dma(sgs[:, lh * 4 + vc0 + mi, al:al + n], o_[:, 0:n], pwrites=wr)
    P.barrier()
    wvb = P.alloc_at(R2, [128, 16, 512], BF16, "wvb")
    vst = [P.alloc_at(R2 + 16 * KB + i * KB, [128, 512], BF16, f"vst{i}") for i in range(4)]
    for lh in range(2):
        P.dma(wvb.v(), wv[lh], q="pool")
        for blk in range(NCH):
            ps = P.ps()
            for k in range(16):
                P.matmul(ps[:, 0:512], aT[:, k, blk * 128:(blk + 1) * 128], wvb[:, k, :], start=(k == 0), stop=(k == 15))
            v_ = vst[(lh * NCH + blk) % 4]
            P.copy(v_.v(), ps[:, 0:512], eng="act" if blk % 2 else "dve")
            P.dma(Vs[blk * 128:(blk + 1) * 128, lh * 512:(lh + 1) * 512], v_.v(), pwrites=[V_c[blk].v()])
    P.barrier()
    P.off = base // 4
    NB = 3
    qb = [P.alloc([128, 4, 128], BF16, f"qb{i}") for i in range(NB)]
    kb = [P.alloc([128, 4, 128], BF16, f"kb{i}") for i in range(NB)]
    vb = [P.alloc([128, 1024], BF16, f"vb{i}") for i in range(NB)]
    ofb = [P.alloc([128, 8, 128], F32, f"ofb{i}") for i in range(NB)]
    sgb = [P.alloc([128, 8, 128], BF16, f"sgb{i}") for i in range(NB)]
    S = [P.alloc([128, 2, 512], F32, f"S{lh}") for lh in range(2)]
    Sb = [P.alloc([128, 2, 512], BF16, f"Sb{lh}") for lh in range(2)]
    attb = [P.alloc([128, 128], BF16, f"attb{i}") for i in range(2)]
    qd = [P.alloc([128, 2, 128], BF16, f"qd{i}") for i in range(2)]
    kd_ = [P.alloc([128, 256], BF16, f"kd{i}") for i in range(2)]
    ost = [P.alloc([128, 4, 128], F32, f"ost{i}") for i in range(2)]
    osq = [P.alloc([128, 4, 128], BF16, f"osq{i}") for i in range(2)]
    rs2 = [P.alloc([128, 128], F32, f"rs2{i}") for i in range(2)]
    mst = [P.alloc([128, 4, 128], BF16, f"mst{i}") for i in range(2)]
    outs = []
    for d in range(2):
        for lh in range(2):
            P.memset(S[lh].v(), 0.0)
            P.memset(Sb[lh].v(), 0.0, eng="pool")
        order = [0, 1] + list(range(2, NCH)) if d == 0 else [1, 0] + list(range(NCH - 1, 1, -1))
        for step, ch in enumerate(order):
            is_ctx = ch < 2
            cl = ch - 2
            bi = step % NB
            q_, k_, v_ = qb[bi], kb[bi], vb[bi]
            P.dma(k_.v(), kT_c[ch].v())
            P.dma(v_.v(), V_c[ch].v())
            if not is_ctx:
                P.dma(q_.v(), qT_c[ch].v())
                if d == 1:
                    P.dma(ofb[bi].v(), of_c[cl].v())
                    P.dma(sgb[bi].v(), sg_c[cl].v())
            for lh in range(2):
                u = (step * 2 + lh) % 2
                if not is_ctx:
                    ps_a = P.ps()
                    for kc in range(2):
                        P.matmul(ps_a[:, 0:128], k_[:, lh * 2 + kc, :], q_[:, lh * 2 + kc, :], start=(kc == 0), stop=(kc == 1))
                    P.tt(attb[u].v(), ps_a[:, 0:128], mkt[:, lh, d, :], ALU.mult)
                    P.tt(qd[u].v(), q_[:, lh * 2:lh * 2 + 2, :], bc_mid(drt[:, lh, d, :], 2), ALU.mult, eng="pool")
                    ps_o = P.ps()
                    for vc in range(4):
                        P.matmul(ps_o[:, vc * 128:(vc + 1) * 128], v_[:, lh * 512 + vc * 128:lh * 512 + (vc + 1) * 128], attb[u].v(),
                                 start=True, stop=False)
                        for kc in range(2):
                            P.matmul(ps_o[:, vc * 128:(vc + 1) * 128], Sb[lh][:, kc, vc * 128:(vc + 1) * 128], qd[u][:, kc, :],
                                     start=False, stop=(kc == 1))
                    if d == 0:
                        P.copy(ost[u].v().rearrange("p a b -> p (a b)"), ps_o[:, 0:512], eng="act")
                        P.dma(ofs[:, lh * 4:(lh + 1) * 4, cl * RC:(cl + 1) * RC], ost[u].v(), pwrites=[of_c[cl].v()])
                    else:
                        o_ = ost[u]
                        P.tt(o_.v().rearrange("p a b -> p (a b)"), ps_o[:, 0:512], ofb[bi][:, lh * 4:(lh + 1) * 4, :].rearrange("p a b -> p (a b)"), ALU.add)
                        P.act(osq[u].v(), o_.v(), AF.Square)
                        ps_n = P.ps()
                        for vc in range(4):
                            P.matmul(ps_n[:, 0:128], C["ones_bf"].v(), osq[u][:, vc, :], start=(vc == 0), stop=(vc == 3))
                        P.act(rs2[u].v(), ps_n[:, 0:128], AF.Sqrt, scale=1.0 / 512, bias=float(RMS_EPS))
                        P.recip(rs2[u].v(), rs2[u].v())
                        P.tt(o_.v(), o_.v(), bc_mid(rs2[u].v(), 4), ALU.mult)
                        P.tt(mst[u].v(), o_.v(), sgb[bi][:, lh * 4:(lh + 1) * 4, :], ALU.mult, eng="pool")
                        outs.append(P.dma(moT[lh * 512:(lh + 1) * 512, cl * RC:(cl + 1) * RC].rearrange("(c p) n -> p c n", p=128), mst[u].v()).idx)
                ps_t = P.ps()
                ptb = ps_t.v().bitcast(BF16)
                for kc in range(2):
                    P.transpose(ptb[:, kc * 128:(kc + 1) * 128], k_[:, lh * 2 + kc, :], C["ident_b"].v())
                P.act(kd_[u].v(), ptb[:, 0:256], AF.Identity, scale=kdt[:, lh, d:d + 1])
                for kc in range(2):
                    ps_s = P.ps()
                    P.matmul(ps_s[:, 0:512], kd_[u][:, kc * 128:(kc + 1) * 128], v_[:, lh * 512:(lh + 1) * 512], start=True, stop=True)
                    P.stt(S[lh][:, kc, :], S[lh][:, kc, :], gct[:, lh:lh + 1], ps_s[:, 0:512], ALU.mult, ALU.add)
                    P.copy(Sb[lh][:, kc, :], S[lh][:, kc, :], eng="act")
    return outs


T_CTX = 256
T_LAT = 4096
T_ALL = T_CTX + T_LAT
CH = 64
NCH = T_ALL // CH
C0 = 0.6065306597126334
GN_EPS = 64e-5


def run_interleaved(gens):
    live = list(gens)
    while live:
        nxt = []
        for g in live:
            try:
                next(g)
                nxt.append(g)
            except StopIteration:
                pass
        live = nxt


def build_rg():
    nc = bass.Bass("TRN2", target_bir_lowering=False)
    I = {}
    I["xT"] = nc.dram_tensor("xT", [D, T_ALL], F32, kind="ExternalInput").ap()
    I["wA"] = nc.dram_tensor("wA", [8, 128, 16, 256], F32, kind="ExternalInput").ap()
    I["wvg"] = nc.dram_tensor("wvg", [128, 16, 256], F32, kind="ExternalInput").ap()
    I["pv"] = nc.dram_tensor("pv", [128, 2, 3, 16], F32, kind="ExternalInput").ap()
    I["rp"] = nc.dram_tensor("rp", [128, 32], F32, kind="ExternalInput").ap()
    I["a2"] = nc.dram_tensor("a2", [96, 256], F32, kind="ExternalInput").ap()
    I["w2"] = nc.dram_tensor("w2", [96, 2, 256], F32, kind="ExternalInput").ap()
    I["g2"] = nc.dram_tensor("g2", [128, 2, 256], F32, kind="ExternalInput").ap()
    I["gku"] = nc.dram_tensor("gku", [16, 2, 128], F32, kind="ExternalInput").ap()
    I["bo"] = nc.dram_tensor("bo", [128, 128], F32, kind="ExternalInput").ap()
    I["msk"] = nc.dram_tensor("msk", [128, 2, 3, 128], F32, kind="ExternalInput").ap()
    I["gmk"] = nc.dram_tensor("gmk", [128, 2, 64], F32, kind="ExternalInput").ap()
    I["rst"] = nc.dram_tensor("rst", [128, 2, 512], F32, kind="ExternalInput").ap()
    moT = nc.dram_tensor("moT", [512, T_ALL], BF16, kind="ExternalOutput").ap()
    P = Prog(nc)
    with ExitStack() as st:
        P.setup_mem(st)
        outs = emit_rg(P, nc, I, moT)
        P.emit(st, final_wait_ops=outs)
    return nc


def emit_rg(P, nc, I, moT, sfx=""):
    KB = 1024
    T = T_ALL

    def scratch(name, shape, dt):
        return nc.dram_tensor(name + sfx, shape, dt, kind="Internal").ap()
    zr = scratch("zr", [128, 10, T], F32)
    gq = scratch("gq", [128, T], F32)
    gk = scratch("gk", [128, T], F32)
    gsg = scratch("gsg", [128, 2, T], BF16)
    ggd = scratch("ggd", [16, T], F32)
    Vg = scratch("Vg", [T, 256], BF16)
    sA = [scratch(f"sA{d}", [128, 2, T], F32) for d in range(2)]
    sB = [scratch(f"sB{d}", [128, 2, T], F32) for d in range(2)]
    sK = [scratch(f"sK{d}", [128, 2, T], F32) for d in range(2)]
    sR = [scratch(f"sR{d}", [128, 2, T], F32) for d in range(2)]
    sV = scratch("sV", [128, 2, T], F32)
    sG = scratch("sG", [128, 2, T], F32)
    sBon = scratch("sBon", [128, 2, T], F32)
    gQ = [scratch(f"gQ{d}", [128, T], BF16) for d in range(2)]
    gK = [scratch(f"gK{d}", [128, T], BF16) for d in range(2)]
    NT256 = T // 256

    def trk(name):
        return [Tl(None, f"{name}{i}") for i in range(NT256)]
    tk = {n: trk(n) for n in ("zr", "gl", "Vg", "rw", "gp")}

    C = make_consts(P)
    pvt = P.alloc([128, 2, 3, 16], F32, "pv")
    P.dma(pvt.v(), I["pv"])
    der = P.alloc([128, 2, 16], F32, "der")
    for kd in range(2):
        P.ts(der[:, kd, :], pvt[:, kd, 1, :], 1.0, ALU.add)
        P.tt(der[:, kd, :], der[:, kd, :], pvt[:, kd, 2, :], ALU.mult)
    rp = P.alloc([128, 32], F32, "rp")
    P.dma(rp.v(), I["rp"])
    dr2 = P.alloc([128, 32], F32, "dr2")
    P.ts(dr2[:, 0:10], rp[:, 0:10], -1.0, ALU.mult, 1.0, ALU.add)
    P.ts(dr2[:, 10:20], rp[:, 0:10], 0.5, ALU.mult)
    P.ts(dr2[:, 20:22], rp[:, 18:20], -1.0, ALU.mult, 1.0, ALU.add)
    P.ts(dr2[:, 22:24], rp[:, 28:30], -1.0, ALU.mult)
    bo = P.alloc([128, 128], F32, "bo")
    P.dma(bo.v(), I["bo"])
    msk = P.alloc([128, 2, 3, 128], F32, "msk")
    P.dma(msk.v(), I["msk"])
    gmk = P.alloc([128, 2, 64], F32, "gmk")
    P.dma(gmk.v(), I["gmk"])
    sq = P.alloc([128, 16, 128], BF16, "sq")
    rstd = P.alloc([128, 128], F32, "rstd")
    pCt = P.alloc([128, 2, 2, NCH], F32, "pCt")
    elast = P.alloc([128, 2, NCH], F32, "elast")
    base = P.off * 4
    assert base <= 16 * KB, base
    aT = P.alloc_at(16 * KB, [128, 16, T], BF16, "aT")
    R2 = 155 * KB
    ht = [P.alloc_at(R2 + i * 8 * KB, [128, 16, 128], F32, f"ht{i}") for i in range(2)]
    tmp = P.alloc_at(R2 + 16 * KB, [128, 16, 128], F32, "tmp")
    for si in range(T // 128):
        a = si * 128
        kd = 1 if a < T_CTX else 0
        h_ = ht[si % 2]
        P.dma(h_.v(), I["xT"][:, a:a + 128].rearrange("(c p) n -> p c n", p=128))
        norm_stats(P, C, h_.v(), 16, 128, sq, rstd, D)
        P.tt(tmp.v(), h_.v(), bc_mid(rstd[:, 0:128], 16), ALU.mult)
        for c in range(16):
            P.act(aT[:, c, a:a + 128], tmp[:, c, :], AF.Identity, scale=der[:, kd, c:c + 1], bias=pvt[:, kd, 0, c:c + 1])
    P.barrier()
    wb = [P.alloc_at(R2 + i * 8 * KB, [128, 16, 256], BF16, f"wb{i}") for i in range(2)]
    stf = [P.alloc_at(R2 + 16 * KB + i * 2 * KB, [128, 512], F32, f"stf{i}") for i in range(4)]
    stb = [P.alloc_at(R2 + 24 * KB + i * KB, [128, 512], BF16, f"stb{i}") for i in range(4)]
    ttl = [(0, 256)] + [(256 + i * 512, 512) for i in range(8)]
    cnt = [0]
    for g in range(8):
        w_ = wb[g % 2]
        P.dma(w_.v(), I["wA"][g], q="pool")
        for mi in range(2):
            ch = g * 2 + mi
            if ch >= 15:
                continue
            for ti, (a, n) in enumerate(ttl):
                ps = P.ps()
                for k in range(16):
                    P.matmul(ps[:, 0:n], w_[:, k, mi * 128:(mi + 1) * 128], aT[:, k, a:a + n], start=(k == 0), stop=(k == 15))
                i0 = cnt[0] % 4
                cnt[0] += 1
                e_ = "act" if cnt[0] % 2 else "dve"
                tks = range(a // 256, (a + n) // 256)
                if ch < 10:
                    P.copy(stf[i0][:, 0:n], ps[:, 0:n], eng=e_)
                    P.dma(zr[:, ch, a:a + n], stf[i0][:, 0:n], pwrites=[tk["zr"][i] for i in tks])
                elif ch == 10:
                    P.act(stf[i0][:, 0:n], ps[:, 0:n], AF.Identity, scale=float(128 ** -0.5))
                    P.dma(gq[:, a:a + n], stf[i0][:, 0:n], pwrites=[tk["gl"][i] for i in tks])
                elif ch == 11:
                    P.copy(stf[i0][:, 0:n], ps[:, 0:n], eng=e_)
                    P.dma(gk[:, a:a + n], stf[i0][:, 0:n], pwrites=[tk["gl"][i] for i in tks])
                elif ch in (12, 13):
                    P.act(stb[i0][:, 0:n], ps[:, 0:n], AF.Silu)
                    P.dma(gsg[:, ch - 12, a:a + n], stb[i0][:, 0:n], pwrites=[tk["gl"][i] for i in tks])
                else:
                    P.copy(stf[i0][0:16, 0:n], ps[0:16, 0:n], eng=e_)
                    P.dma(ggd[:, a:a + n], stf[i0][0:16, 0:n], pwrites=[tk["gl"][i] for i in tks])
    P.barrier()
    wvb = P.alloc_at(R2, [128, 16, 256], BF16, "wvb")
    vst = [P.alloc_at(R2 + 8 * KB + i * KB, [128, 256], BF16, f"vst{i}") for i in range(4)]
    P.dma(wvb.v(), I["wvg"], q="pool")
    for blk in range(T // 128):
        ps = P.ps()
        for k in range(16):
            P.matmul(ps[:, 0:256], aT[:, k, blk * 128:(blk + 1) * 128], wvb[:, k, :], start=(k == 0), stop=(k == 15))
        v_ = vst[blk % 4]
        P.copy(v_.v(), ps[:, 0:256], eng="act" if blk % 2 else "dve")
        P.dma(Vg[blk * 128:(blk + 1) * 128, :], v_.v(), pwrites=[tk["Vg"][blk // 2]])
    P.barrier()
    P.off = base // 4
    a2t = P.alloc([96, 256], F32, "a2t")
    P.dma(a2t.v(), I["a2"])
    w2t = P.alloc([96, 2, 256], F32, "w2t")
    P.dma(w2t.v(), I["w2"])
    g2t = P.alloc([128, 2, 256], F32, "g2t")
    P.dma(g2t.v(), I["g2"])
    gkut = P.alloc([16, 2, 128], F32, "gkut")
    P.dma(gkut.v(), I["gku"])
    rst = P.alloc([128, 2, 512], F32, "rst")
    P.dma(rst.v(), I["rst"])
    n = 256
    zt = [P.alloc([128, 10, n + 2], F32, f"zt{i}") for i in range(2)]
    tsum = P.alloc([128, 10, n], F32, "tsum")
    zs = P.alloc([128, 10, n], F32, "zs")
    lr = P.alloc([128, 2, n], F32, "lr")
    tw = P.alloc([96, n], F32, "tw")
    sig = P.alloc([128, 2, 2, n], F32, "sig")
    sgd = P.alloc([128, 2, n], F32, "sgd")
    gst = P.alloc([128, 2, n], F32, "gst")
    kk = P.alloc([128, 2, n], F32, "kk")
    ksq = P.alloc([128, 2, n], F32, "ksq")
    rinv = P.alloc([128, 2, n], F32, "rinv")
    fac = P.alloc([128, 2, n], F32, "fac")
    kp = P.alloc([128, 2, n], F32, "kp")
    bb = P.alloc([128, 2, n], F32, "bb")
    rk = P.alloc([128, 2, n], F32, "rk")
    bon = P.alloc([128, 2, n], F32, "bon")
    cs = P.alloc([128, 2, n], F32, "cs")
    e_inv = P.alloc([128, 2, n], F32, "e_inv")
    e_inc = P.alloc([128, 2, n], F32, "e_inc")
    e_exc = P.alloc([128, 2, n], F32, "e_exc")
    oA = [P.alloc([128, 2, n], F32, f"oA{i}") for i in range(2)]
    oB = [P.alloc([128, 2, n], F32, f"oB{i}") for i in range(2)]
    oK = [P.alloc([128, 2, n], F32, f"oK{i}") for i in range(2)]
    oR = [P.alloc([128, 2, n], F32, f"oR{i}") for i in range(2)]
    gdt = P.alloc([16, n], F32, "gdt")
    gqt = P.alloc([128, n], F32, "gqt")
    gkt = P.alloc([128, n], F32, "gkt")
    gsp = P.alloc([128, n], F32, "gsp")
    gcs = P.alloc([128, n], F32, "gcs")
    ge1 = P.alloc([128, n], F32, "ge1")
    ge2 = P.alloc([128, n], F32, "ge2")
    gqo = [P.alloc([128, n], BF16, f"gqo{i}") for i in range(2)]
    gko = [P.alloc([128, n], BF16, f"gko{i}") for i in range(2)]

    def fl(v):
        return v.rearrange("p a b -> p (a b)")
    for ti in range(NT256):
        a = ti * n
        z_ = zt[ti % 2]
        lz = (a == 0) or (a == T_CTX)
        rz = (a + n == T_CTX) or (a + n == T)
        rd = [tk["zr"][i] for i in range(max(0, ti - 1), min(NT256, ti + 2))]
        if lz:
            P.memset(z_[:, :, 0:1], 0.0, eng="pool")
        if rz:
            P.memset(z_[:, :, n + 1:n + 2], 0.0, eng="pool")
        lo = a if lz else a - 1
        hi = a + n if rz else a + n + 1
        P.dma(z_[:, :, (lo - a + 1):(hi - a + 1)], zr[:, :, lo:hi], reads=rd)
        P.tt(tsum.v(), z_[:, :, 0:n], z_[:, :, 2:n + 2], ALU.add, eng="pool")
        P.tt(tsum.v(), tsum.v(), bc_inner(dr2[:, 10:20], n), ALU.mult, eng="pool")
        P.tt(zs.v(), z_[:, :, 1:n + 1], bc_inner(dr2[:, 0:10], n), ALU.mult)
        P.tt(zs.v(), zs.v(), tsum.v(), ALU.add)
        wrk = [tk["rw"][ti]]
        P.dma(sV[:, :, a:a + n], zs[:, 4:6, :], pwrites=wrk)
        for c in range(2):
            ps = P.ps()
            P.matmul(ps[:, 0:n], a2t[0:96, c * 128:(c + 1) * 128], zs[0:96, 7, :])
            P.act(lr[:, c, :], ps[:, 0:n], AF.Sigmoid, bias=rp[:, 10 + c:11 + c])
        P.act(tw.v(), zs[0:96, 6, :], AF.Tanh)
        for d in range(2):
            for c in range(2):
                ps = P.ps()
                P.matmul(ps[:, 0:n], w2t[0:96, d, c * 128:(c + 1) * 128], tw.v())
                P.act(sig[:, d, c, :], ps[:, 0:n], AF.Sigmoid, bias=rp[:, 12 + d * 2 + c:13 + d * 2 + c])
        P.act(sgd.v(), zs[:, 8:10, :], AF.Sigmoid)
        for c in range(2):
            ps = P.ps()
            for kc in range(2):
                P.matmul(ps[:, 0:n], g2t[:, kc, c * 128:(c + 1) * 128], sgd[:, kc, :], start=(kc == 0), stop=(kc == 1))
            P.copy(gst[:, c, :], ps[:, 0:n], eng="act")
        P.dma(sG[:, :, a:a + n], gst.v(), pwrites=wrk)
        P.tt(kk.v(), zs[:, 2:4, :], bc_inner(rp[:, 16:18], n), ALU.mult)
        P.act(ksq.v(), kk.v(), AF.Square)
        for c in range(2):
            ps = P.ps()
            P.matmul(ps[:, 0:n], bo.v(), ksq[:, c, :])
            P.act(rinv[:, c, :], ps[:, 0:n], AF.Sqrt)
        P.ts(rinv.v(), rinv.v(), 1e-12, ALU.max)
        P.recip(fl(rinv.v()), fl(rinv.v()))
        P.tt(kk.v(), kk.v(), rinv.v(), ALU.mult)
        for c in range(2):
            P.ts(fac[:, c, :], lr[:, c, :], rp[:, 18 + c:19 + c], ALU.mult, dr2[:, 20 + c:21 + c], ALU.add)
        P.tt(kp.v(), zs[:, 2:4, :], fac.v(), ALU.mult)
        P.tt(bb.v(), kk.v(), lr.v(), ALU.mult)
        P.tt(rk.v(), zs[:, 0:2, :], kp.v(), ALU.mult, eng="pool")
        P.tt(rk.v(), rk.v(), bc_inner(rp[:, 20:22], n), ALU.mult, eng="pool")
        for c in range(2):
            ps = P.ps()
            P.matmul(ps[:, 0:n], bo.v(), rk[:, c, :])
            P.tt(bon[:, c, :], ps[:, 0:n], zs[:, 4 + c, :], ALU.mult)
        P.dma(sBon[:, :, a:a + n], bon.v(), pwrites=wrk)
        nch = n // CH
        ch0 = a // CH
        for d in range(2):
            sgv = fl(sig[:, d, :, :])
            if d == 0:
                P.scan(fl(cs.v()), rst[:, 0, 0:2 * n], sgv, 0.0, ALU.mult, ALU.add)
            else:
                co = fl(cs.v()).ap[:, ::-1]
                si_ = sgv.ap[:, ::-1]
                rs_ = rst.ap[:, 1, 0:2 * n][:, ::-1]
                P.raw("dve", lambda e, co=co, si_=si_, rs_=rs_: e.tensor_tensor_scan(co, rs_, si_, 0.0, ALU.mult, ALU.add),
                      [sig.v(), rst.v()], [cs.v()])
            P.act(e_inv.v(), cs.v(), AF.Exp, scale=C0)
            P.act(e_inc.v(), cs.v(), AF.Exp, scale=-C0)
            P.tt(e_exc.v(), cs.v(), sig[:, d, :, :], ALU.subtract, eng="pool")
            P.act(e_exc.v(), e_exc.v(), AF.Exp, scale=-C0)
            o_ = d
            P.stt(oA[o_].v(), kk.v(), -1.0, e_exc.v(), ALU.mult, ALU.mult)
            P.tt(oB[o_].v(), bb.v(), e_inv.v(), ALU.mult)
            P.tt(oK[o_].v(), kp.v(), e_inv.v(), ALU.mult, eng="pool")
            P.tt(oR[o_].v(), zs[:, 0:2, :], e_inc.v(), ALU.mult, eng="pool")
            col = CH - 1 if d == 0 else 0
            P.copy(pCt[:, d, :, ch0:ch0 + nch], e_inc.v().rearrange("p c (k t) -> p c k t", t=CH)[:, :, :, col], eng="act")
            P.dma(sA[d][:, :, a:a + n], oA[o_].v(), pwrites=wrk)
            P.dma(sB[d][:, :, a:a + n], oB[o_].v(), pwrites=wrk)
            P.dma(sK[d][:, :, a:a + n], oK[o_].v(), pwrites=wrk)
            P.dma(sR[d][:, :, a:a + n], oR[o_].v(), pwrites=wrk)
        rdg = [tk["gl"][ti]]
        P.dma(gdt.v(), ggd[:, a:a + n], reads=rdg)
        P.dma(gqt.v(), gq[:, a:a + n], reads=rdg)
        P.dma(gkt.v(), gk[:, a:a + n], reads=rdg)
        for d in range(2):
            ps = P.ps()
            P.matmul(ps[:, 0:n], gkut[0:16, d, :], gdt.v())
            P.act(gsp.v(), ps[:, 0:n], AF.Exp, scale=-1.0, bias=dr2[:, 22 + d:23 + d])
            P.act(gsp.v(), gsp.v(), AF.Ln, bias=1.0)
            if d == 0:
                P.scan(gcs.v(), rst[:, 0, 0:n], gsp.v(), 0.0, ALU.mult, ALU.add)
            else:
                co = gcs.ap[:, ::-1]
                si_ = gsp.ap[:, ::-1]
                rs_ = rst.ap[:, 1, 0:n][:, ::-1]
                P.raw("dve", lambda e, co=co, si_=si_, rs_=rs_: e.tensor_tensor_scan(co, rs_, si_, 0.0, ALU.mult, ALU.add),
                      [gsp.v(), rst.v()], [gcs.v()])
            P.act(ge1.v(), gcs.v(), AF.Exp, scale=-1.0 / 16)
            P.act(ge2.v(), gcs.v(), AF.Exp, scale=1.0 / 16)
            P.tt(gqo[d].v(), gqt.v(), ge1.v(), ALU.mult)
            P.tt(gko[d].v(), gkt.v(), ge2.v(), ALU.mult, eng="pool")
            col = CH - 1 if d == 0 else 0
            P.copy(elast[:, d, ch0:ch0 + nch], ge1.v().rearrange("p (k t) -> p k t", t=CH)[:, :, col], eng="act")
            P.dma(gQ[d][:, a:a + n], gqo[d].v(), pwrites=[tk["gp"][ti]])
            P.dma(gK[d][:, a:a + n], gko[d].v(), pwrites=[tk["gp"][ti]])
    P.barrier()
    P.off = base // 4
    off_e = P.off
    ya = P.alloc([128, 2, T], F32, "ya")
    P.memset(ya[:, 0, :], 0.0)
    P.memset(ya[:, 1, :], 0.0, eng="pool")
    P.barrier()
    ya_c = [[Tl(ya.ap[:, p, c * CH:(c + 1) * CH], f"ya{p}_{c}") for c in range(NCH)] for p in range(2)]
    idf = C["ident_f"]
    NB = 2
    slots = []
    for s in range(4):
        sl = {}
        for gen in range(NB):
            for nm in ("Ablk", "Bblk", "Kblk", "Vblk"):
                t = P.alloc([128, 128], F32, f"{nm}{s}{gen}")
                P.memset(t.v(), 0.0, eng="pool" if (gen % 2) else "dve")
                sl[(nm, gen)] = t
            for nm in ("A", "B", "R"):
                sl[(nm, gen)] = P.alloc([128, CH], F32, f"{nm}{s}{gen}")
        for nm in ("S0", "S0T", "LakT", "Atb", "Btb", "Ktb", "Vtb", "Ta", "Tb", "Sa", "SaT", "Sb", "SbT",
                   "W1", "Xtb", "Ahat", "GpT", "Hadd"):
            sl[nm] = P.alloc([128, 128], F32, f"{nm}{s}")
        for nm in ("ArbT", "ArkT", "Rhat"):
            sl[nm] = P.alloc([128, CH], F32, f"{nm}{s}")
        sl["H"] = P.alloc([128, 128], F32, f"H{s}")
        slots.append(sl)
    orders = [list(range(NCH)), [3, 2, 1, 0] + list(range(NCH - 1, 3, -1))]

    def unit(s, pair, d, step, ch):
        sl = slots[s]
        gen = step % NB
        Ablk, Bblk, Kblk, Vblk = (sl[(nm, gen)] for nm in ("Ablk", "Bblk", "Kblk", "Vblk"))
        A_, B_, R_ = (sl[(nm, gen)] for nm in ("A", "B", "R"))
        tsl = slice(ch * CH, (ch + 1) * CH)
        rd = [tk["rw"][ch // 4]]
        for (blk, src) in ((Ablk, sA[d]), (Bblk, sB[d]), (Kblk, sK[d]), (Vblk, sV)):
            P.dma(blk[0:64, 0:64], src[0:64, pair, tsl], reads=rd)
            P.dma(blk[64:128, 64:128], src[64:128, pair, tsl], reads=rd)
        P.dma(A_.v(), sA[d][:, pair, tsl], reads=rd)
        P.dma(B_.v(), sB[d][:, pair, tsl], reads=rd)
        P.dma(R_.v(), sR[d][:, pair, tsl], reads=rd)
        yield
        Ms, MsT, MiT = msk[:, d, 0, :], msk[:, d, 1, :], msk[:, d, 2, 0:CH]

        def mm_ev(dst, lhsT, rhs, N, mask=None, eng="dve", add=None):
            ps = P.ps()
            P.matmul(ps[:, 0:N], lhsT, rhs)
            if mask is not None:
                P.tt(dst, ps[:, 0:N], mask, ALU.mult, eng="dve")
            elif add is not None:
                P.tt(dst, ps[:, 0:N], add, ALU.add, eng="dve")
            else:
                P.copy(dst, ps[:, 0:N], eng=eng)
        def dup(t):
            ap = t.ap
            (ps_, pn), (s1, n1) = ap.ap
            return Vw(t, bass.AP(ap.tensor, ap.offset, [[ps_, pn], [0, 2], [s1, n1]]))
        Ad, Bd = dup(A_), dup(B_)
        ps = P.ps()
        P.matmul(ps[:, 0:128].rearrange("p (a b) -> p a b", a=2), Bblk.v(), Ad)
        P.tt(sl["S0T"].v(), ps[:, 0:128], MsT, ALU.mult)
        ps = P.ps()
        P.matmul(ps[:, 0:128].rearrange("p (a b) -> p a b", a=2), Ablk.v(), Bd)
        P.tt(sl["S0"].v(), ps[:, 0:128], Ms, ALU.mult)
        yield
        ps = P.ps()
        P.matmul(ps[:, 0:128].rearrange("p (a b) -> p a b", a=2), Kblk.v(), Ad)
        P.tt(sl["LakT"].v(), ps[:, 0:128], MsT, ALU.mult)
        mm_ev(sl["ArbT"].v(), Bblk.v(), R_.v(), CH, mask=MiT)
        mm_ev(sl["ArkT"].v(), Kblk.v(), R_.v(), CH, mask=MiT)
        yield
        for (dst, src) in (("Atb", Ablk), ("Btb", Bblk), ("Ktb", Kblk), ("Vtb", Vblk)):
            ps = P.ps()
            P.transpose(ps[:, 0:128], src.v(), idf.v())
            P.copy(sl[dst].v(), ps[:, 0:128], eng="act")
        P.tt(sl["Ta"].v(), sl["S0T"].v(), idf.v(), ALU.add, eng="pool")
        yield
        S, ST = sl["S0"], sl["S0T"]
        Tcur, Tnxt = sl["Ta"], sl["Tb"]
        pp = [(sl["Sa"], sl["SaT"]), (sl["Sb"], sl["SbT"])]
        for lvl in range(5):
            Sn, SnT = pp[lvl % 2]
            mm_ev(Sn.v(), ST.v(), S.v(), 128, eng="act")
            if lvl < 4:
                mm_ev(SnT.v(), S.v(), ST.v(), 128, eng="act")
            yield
            mm_ev(Tnxt.v(), Sn.v(), Tcur.v(), 128, add=Tcur.v())
            Tcur, Tnxt = Tnxt, Tcur
            S, ST = Sn, SnT
            yield
        TT = Tcur
        mm_ev(sl["W1"].v(), sl["LakT"].v(), sl["Vtb"].v(), 128, eng="act")
        mm_ev(sl["Ahat"].v(), TT.v(), sl["Atb"].v(), 128, eng="act")
        yield
        mm_ev(sl["Xtb"].v(), TT.v(), sl["W1"].v(), 128, eng="act")
        mm_ev(sl["GpT"].v(), sl["Ahat"].v(), sl["Btb"].v(), 128, add=idf.v())
        mm_ev(sl["Rhat"].v(), sl["Ahat"].v(), sl["ArbT"].v(), CH, add=R_.v())
        yield
        pc = pCt[:, d, pair, ch:ch + 1]
        ps = P.ps()
        P.matmul(ps[:, 0:128], sl["Btb"].v(), sl["Xtb"].v(), start=True, stop=False)
        P.matmul(ps[:, 0:128], sl["Ktb"].v(), sl["Vtb"].v(), start=False, stop=True)
        P.ts(sl["Hadd"].v(), ps[:, 0:128], pc, ALU.mult)
        yield
        H = sl["H"]
        ps = P.ps()
        P.matmul(ps[:, 0:CH], sl["Vtb"].v(), sl["ArkT"].v(), start=True, stop=False)
        P.matmul(ps[:, 0:CH], sl["Xtb"].v(), sl["ArbT"].v(), start=False, stop=False)
        P.matmul(ps[:, 0:CH], H.v(), sl["Rhat"].v(), start=False, stop=True)
        yc = ya_c[pair][ch]
        P.tt(yc.v(), yc.v(), ps[:, 0:CH], ALU.add)
        ps = P.ps()
        P.matmul(ps[:, 0:128], sl["GpT"].v(), H.v())
        P.stt(H.v(), ps[:, 0:128], pc, sl["Hadd"].v(), ALU.mult, ALU.add)
        yield

    for s in range(4):
        P.memset(slots[s]["H"].v(), 0.0)
    for step in range(NCH):
        gens = []
        for s, (pair, d) in enumerate(((0, 0), (1, 0), (0, 1), (1, 1))):
            gens.append(unit(s, pair, d, step, orders[d][step]))
        run_interleaved(gens)
    P.barrier()
    P.off = off_e + (2 * T * 4 // 4 + 7) // 8 * 8
    yb = P.alloc([128, 2, T], F32, "yb")
    P.memset(yb[:, 0, :], 0.0)
    P.memset(yb[:, 1, :], 0.0, eng="pool")
    P.barrier()
    yb_c = [Tl(yb.ap[:, :, c * CH:(c + 1) * CH], f"yb{c}") for c in range(NCH)]
    NBG = 3
    gqb = [[P.alloc([128, CH], BF16, f"gqb{d}{i}") for i in range(NBG)] for d in range(2)]
    gkb_ = [[P.alloc([128, CH], BF16, f"gkb{d}{i}") for i in range(NBG)] for d in range(2)]
    gvb = [[P.alloc([CH, 256], BF16, f"gvb{d}{i}") for i in range(NBG)] for d in range(2)]
    gS = [P.alloc([128, 256], F32, f"gS{d}") for d in range(2)]
    gSb = [P.alloc([128, 256], BF16, f"gSb{d}") for d in range(2)]
    gat = [P.alloc([CH, CH], BF16, f"gat{d}") for d in range(2)]
    gkt_ = [P.alloc([CH, 128], BF16, f"gkt{d}") for d in range(2)]
    gtm = [P.alloc([128, 256], F32, f"gtm{d}") for d in range(2)]
    idb = C["ident_b"]

    def gunit(d, step, ch):
        bi = step % NBG
        q_, k_, v_ = gqb[d][bi], gkb_[d][bi], gvb[d][bi]
        tsl = slice(ch * CH, (ch + 1) * CH)
        rd = [tk["gp"][ch // 4]]
        P.dma(q_.v(), gQ[d][:, tsl], reads=rd)
        P.dma(k_.v(), gK[d][:, tsl], reads=rd)
        P.dma(v_.v(), Vg[tsl, :], reads=[tk["Vg"][ch // 4]])
        yield
        ps = P.ps()
        P.matmul(ps[0:CH, 0:CH], k_.v(), q_.v())
        P.tt(gat[d].v(), ps[0:CH, 0:CH], gmk[0:CH, d, :], ALU.mult)
        pst = P.ps()
        ptb = pst.v().bitcast(BF16)
        P.transpose(ptb[0:CH, 0:128], k_.v(), idb.v())
        P.copy(gkt_[d].v(), ptb[0:CH, 0:128], eng="act")
        yield
        pso = P.ps()
        for vc in range(2):
            P.matmul(pso[:, vc * CH:(vc + 1) * CH], v_[:, vc * 128:(vc + 1) * 128], gat[d].v(), start=True, stop=False)
            P.matmul(pso[:, vc * CH:(vc + 1) * CH], gSb[d][:, vc * 128:(vc + 1) * 128], q_.v(), start=False, stop=True)
        yc = yb_c[ch]
        P.tt(yc.v(), yc.v(), pso[:, 0:2 * CH].rearrange("p (a b) -> p a b", a=2), ALU.add)
        yield
        pss = P.ps()
        P.matmul(pss[:, 0:256], gkt_[d].v(), v_.v())
        el = elast[:, d, ch:ch + 1]
        P.ts(gtm[d].v(), gS[d].v(), el, ALU.mult, eng="pool")
        P.stt(gS[d].v(), pss[:, 0:256], el, gtm[d].v(), ALU.mult, ALU.add)
        P.copy(gSb[d].v(), gS[d].v(), eng="act")
        yield
    for d in range(2):
        P.memset(gS[d].v(), 0.0)
        P.memset(gSb[d].v(), 0.0, eng="pool")
    for step in range(NCH):
        run_interleaved([gunit(d, step, orders[d][step]) for d in range(2)])
    P.barrier()
    n = 256
    fg = [P.alloc([128, 2, n], F32, f"fg{i}") for i in range(2)]
    fbn = [P.alloc([128, 2, n], F32, f"fbn{i}") for i in range(2)]
    fsg = [P.alloc([128, 2, n], BF16, f"fsg{i}") for i in range(2)]
    ysq = P.alloc([128, 2, n], F32, "ysq")
    mean = P.alloc([128, 2, n], F32, "mean")
    var = P.alloc([128, 2, n], F32, "var")
    yn = P.alloc([128, 2, n], F32, "yn")
    oa = [P.alloc([128, 2, n], BF16, f"oa{i}") for i in range(2)]
    bsq = P.alloc([128, 2, n], BF16, "bsq")
    brs = P.alloc([128, n], F32, "brs")
    ybn = P.alloc([128, 2, n], F32, "ybn")
    obb = [P.alloc([128, 2, n], BF16, f"obb{i}") for i in range(2)]
    outs = []
    for ti in range(NT256):
        a = ti * n
        i2 = ti % 2
        P.dma(fg[i2].v(), sG[:, :, a:a + n], reads=[tk["rw"][ti]])
        P.dma(fbn[i2].v(), sBon[:, :, a:a + n], reads=[tk["rw"][ti]])
        P.dma(fsg[i2].v(), gsg[:, :, a:a + n], reads=[tk["gl"][ti]])
        yv = ya[:, :, a:a + n]
        P.act(ysq.v(), yv, AF.Square)
        for c in range(2):
            ps = P.ps()
            P.matmul(ps[:, 0:n], bo.v(), ya[:, c, a:a + n])
            P.act(mean[:, c, :], ps[:, 0:n], AF.Identity, scale=1.0 / 64)
            ps = P.ps()
            P.matmul(ps[:, 0:n], bo.v(), ysq[:, c, :])
            P.act(var[:, c, :], ps[:, 0:n], AF.Identity, scale=1.0 / 64)
        P.tt(yn.v(), mean.v(), mean.v(), ALU.mult, eng="pool")
        P.tt(var.v(), var.v(), yn.v(), ALU.subtract, eng="pool")
        P.act(var.v(), var.v(), AF.Sqrt, bias=float(GN_EPS))
        P.recip(fl(var.v()), fl(var.v()))
        P.tt(yn.v(), yv, mean.v(), ALU.subtract)
        P.tt(yn.v(), yn.v(), var.v(), ALU.mult)
        for c in range(2):
            P.ts(yn[:, c, :], yn[:, c, :], rp[:, 22 + c:23 + c], ALU.mult, rp[:, 24 + c:25 + c], ALU.add)
        P.tt(yn.v(), yn.v(), fbn[i2].v(), ALU.add)
        P.tt(oa[i2].v(), yn.v(), fg[i2].v(), ALU.mult)
        outs.append(P.dma(moT[0:256, a:a + n].rearrange("(c p) n -> p c n", p=128), oa[i2].v()).idx)
        ybv = yb[:, :, a:a + n]
        P.act(bsq.v(), ybv, AF.Square)
        ps = P.ps()
        for c in range(2):
            P.matmul(ps[:, 0:n], C["ones_bf"].v(), bsq[:, c, :], start=(c == 0), stop=(c == 1))
        P.act(brs.v(), ps[:, 0:n], AF.Sqrt, scale=1.0 / 256, bias=float(RMS_EPS))
        P.recip(brs.v(), brs.v())
        P.tt(ybn.v(), ybv, bc_mid(brs.v(), 2), ALU.mult)
        for c in range(2):
            P.stt(obb[i2][:, c, :], ybn[:, c, :], rp[:, 26 + c:27 + c], fsg[i2][:, c, :], ALU.mult, ALU.mult)
        outs.append(P.dma(moT[256:512, a:a + n].rearrange("(c p) n -> p c n", p=128), obb[i2].v()).idx)
    return outs


BF = ml_dtypes.bfloat16

def tile_w(W, ncg):
    K, Cc = W.shape
    KC = K // 128
    G = Cc // ncg
    return np.ascontiguousarray(W.reshape(KC, 128, G, ncg).transpose(2, 1, 0, 3))

def vecT(v):
    n = v.shape[-1] // 128
    return np.ascontiguousarray(v.reshape(n, 128).T)

def tile_wi(W):
    Wg, Wu = W[:, :5632], W[:, 5632:]
    cat = np.concatenate([Wg.reshape(2048, 22, 256), Wu.reshape(2048, 22, 256)], axis=2).reshape(2048, 22 * 512)
    return tile_w(cat, 512)

def post_pv(mod3, ng, b):
    out = np.zeros((128, 2, 7, 16), np.float32)
    for kd, row in enumerate((b, 2)):
        m = mod3[row].reshape(6, 2048)
        vs = [m[2], m[3], m[4], m[5], ng[1], ng[2], ng[3]]
        for i, v in enumerate(vs):
            out[:, kd, i, :] = vecT(v)
    return out

RET_GAMMA = [1.0 - 2.0 ** (-5.0 - h) for h in range(8)]

def ret_consts(hp, C=128):
    mk = np.zeros((128, 2, 2, C), np.float32)
    dr = np.zeros((128, 2, 2, C), np.float32)
    kdc = np.zeros((128, 2, 2), np.float32)
    gC = np.zeros((128, 2), np.float32)
    idx = np.arange(C)
    for lh in range(2):
        g = np.float64(np.float32(RET_GAMMA[2 * hp + lh]))
        lg = np.log(np.float32(g)).astype(np.float64)
        diff = idx[None, :] - idx[:, None]
        mk[:, lh, 0, :] = np.where(diff >= 0, np.exp(diff * lg), 0.0)
        mk[:, lh, 1, :] = np.where(diff <= 0, np.exp(-diff * lg), 0.0)
        dr[:, lh, 0, :] = np.exp((idx + 1.0) * lg)[None, :]
        dr[:, lh, 1, :] = np.exp((C - idx) * lg)[None, :]
        kdc[:, lh, 0] = np.exp((C - 1.0 - idx) * lg)
        kdc[:, lh, 1] = np.exp(idx * lg)
        gC[:, lh] = np.exp(C * lg)
    return mk, dr, kdc, gC

def rot_tables():
    nf = 64
    inv = (10000.0 ** (-np.arange(nf, dtype=np.float32) / nf)).astype(np.float32)
    rows = np.repeat(np.arange(64, dtype=np.float32), 64)
    cols = np.tile(np.arange(64, dtype=np.float32), 64)
    ang = np.concatenate([rows[:, None] * inv, cols[:, None] * inv], axis=-1).astype(np.float32)
    cs = np.stack([np.cos(ang.astype(np.float64)).T, np.sin(ang.astype(np.float64)).T], 1).astype(np.float32)
    return np.ascontiguousarray(cs)

def ret_inputs(hT_b, w_in, mod3, ng, b, hp, cs):
    h0, h1 = 2 * hp, 2 * hp + 1
    sel = []
    for h in (h0, h1):
        sel.append(w_in[:, h * 256:(h + 1) * 256])
    for h in (h0, h1):
        sel.append(w_in[:, 2048 + h * 256:2048 + (h + 1) * 256])
    for h in (h0, h1):
        sel.append(w_in[:, 8192 + h * 512:8192 + h * 512 + 256])
        sel.append(w_in[:, 8192 + h * 512 + 256:8192 + (h + 1) * 512])
    wqkg = np.stack([tile_w(s, 256)[0] for s in sel], 0)
    wv = np.stack([tile_w(w_in[:, 4096 + h * 512:4096 + (h + 1) * 512], 512)[0] for h in (h0, h1)], 0)
    pv = np.zeros((128, 2, 3, 16), np.float32)
    for kd, row in enumerate((b, 2)):
        m = mod3[row].reshape(6, 2048)
        for i, v in enumerate((m[0], m[1], ng[0])):
            pv[:, kd, i, :] = vecT(v)
    mk, dr, kdc, gC = ret_consts(hp)
    return {"hT": hT_b, "wqkg": wqkg, "wv": wv, "pv": pv, "cs": cs, "mk": mk, "dr": dr, "kdc": kdc, "gC": gC}

def rg_consts():
    bo = np.zeros((128, 128), np.float32)
    bo[:64, :64] = 1; bo[64:, 64:] = 1
    msk = np.zeros((128, 2, 3, 128), np.float32)
    i = np.arange(64)
    lt = (i[:, None] > i[None, :]).astype(np.float32)
    le = (i[:, None] <= i[None, :]).astype(np.float32)
    for h in range(2):
        sl = slice(h * 64, (h + 1) * 64)
        msk[sl, 0, 0, sl] = lt
        msk[sl, 0, 1, sl] = lt.T
        msk[sl, 0, 2, 0:64] = le
        msk[sl, 1, 0, sl] = lt.T
        msk[sl, 1, 1, sl] = lt
        msk[sl, 1, 2, 0:64] = le.T
    gmk = np.zeros((128, 2, 64), np.float32)
    gmk[:64, 0, :] = le
    gmk[:64, 1, :] = le.T
    rst = np.ones((128, 2, 512), np.float32)
    rst[:, 0, 0::64] = 0
    rst[:, 1, 63::64] = 0
    return {"bo": bo, "msk": msk, "gmk": gmk, "rst": rst}

def pad128(w):
    if w.shape[1] == 128:
        return w
    o = np.zeros((w.shape[0], 128), np.float32)
    o[:, :w.shape[1]] = w
    return o

def vec2(v):
    return np.ascontiguousarray(v.reshape(2, 128).T)

def vecpad(v):
    o = np.zeros(128, np.float32)
    o[:v.shape[0]] = v
    return o

def rg_inputs(xT_b, inp, mod3, ng, b, g, consts):
    w = inp["ab_w_in"][0]
    A0 = 3520
    o = 256 * g
    ch = []
    for base in (0, 1024, 2048):
        ch += [w[:, base + o:base + o + 128], w[:, base + o + 128:base + o + 256]]
    ch.append(pad128(w[:, 3072:3168]))
    ch.append(pad128(w[:, 3168:3264]))
    ch += [w[:, 3264:3392], w[:, 3392:3520]]
    ch.append(w[:, A0 + 128 * g:A0 + 128 * g + 128])
    ch.append(w[:, A0 + 512 + 128 * g:A0 + 512 + 128 * g + 128])
    ch += [w[:, A0 + 2048 + o:A0 + 2048 + o + 128], w[:, A0 + 2048 + o + 128:A0 + 2048 + o + 256]]
    ch.append(pad128(w[:, A0 + 3072:A0 + 3088]))
    ch.append(np.zeros((2048, 128), np.float32))
    wA = tile_w(np.concatenate(ch, 1), 256)
    wvg = tile_w(w[:, A0 + 1024 + o:A0 + 1024 + o + 256], 256)[0]
    pv = np.zeros((128, 2, 3, 16), np.float32)
    for kd, row in enumerate((b, 2)):
        m = mod3[row].reshape(6, 2048)
        for i, v in enumerate((m[0], m[1], ng[0])):
            pv[:, kd, i, :] = vecT(v)
    mu = inp["rwkv_mu"][0]
    rp = np.zeros((128, 32), np.float32)
    for i, base in enumerate((0, 1024, 2048)):
        rp[:, 2 * i:2 * i + 2] = vec2(mu[base + o:base + o + 256])
    rp[:, 6] = vecpad(mu[3072:3168]); rp[:, 7] = vecpad(mu[3168:3264])
    rp[:, 8:10] = vec2(mu[3264:3520])
    own = slice(o, o + 256)
    rp[:, 10:12] = vec2(inp["rwkv_a0"][0][own])
    for d in range(2):
        rp[:, 12 + 2 * d:14 + 2 * d] = vec2(inp["rwkv_w0"][0][d][own])
    rp[:, 16:18] = vec2(inp["rwkv_kk"][0][own])
    rp[:, 18:20] = vec2(inp["rwkv_ka"][0][own])
    rp[:, 20:22] = vec2(inp["rwkv_rk"][0].reshape(-1)[own])
    rp[:, 22:24] = vec2(inp["rwkv_ln_w"][0][own])
    rp[:, 24:26] = vec2(inp["rwkv_ln_b"][0][own])
    rp[:, 26:28] = vec2(inp["gla_norm_g"][0])
    for d in range(2):
        rp[:, 28 + d] = inp["gla_gk_b"][0][d][128 * g:128 * g + 128]
    a2 = np.ascontiguousarray(inp["rwkv_a2"][0][:, own])
    w2 = np.ascontiguousarray(np.stack([inp["rwkv_w2"][0][d][:, own] for d in range(2)], 1))
    g2 = np.ascontiguousarray(inp["rwkv_g2"][0][:, own].reshape(2, 128, 256).transpose(1, 0, 2))
    gku = np.ascontiguousarray(np.stack([inp["gla_gk_up"][0][d][:, 128 * g:128 * g + 128] for d in range(2)], 1))
    d_ = {"xT": xT_b, "wA": wA, "wvg": wvg, "pv": pv, "rp": rp, "a2": a2, "w2": w2, "g2": g2, "gku": gku}
    d_.update(consts)
    return d_


def _mods_inputs(inp, core):
    cs = np.stack([inp["c"][0], inp["c"][1], inp["c_ctx"]], 0)
    cT = np.ascontiguousarray(cs.reshape(3, 16, 128).transpose(2, 1, 0))
    cols = slice(core * 1536, (core + 1) * 1536)
    mw = np.stack([tile_w(inp["mod_w"][l][:, cols], 1536)[0] for l in range(2)], 0)
    mb = np.stack([vecT(inp["mod_b"][l][cols]) for l in range(2)], 1)
    return {"cT": cT, "mw": mw, "mb": np.ascontiguousarray(mb)}


def _run(nc, in_maps):
    res = run_bass_kernel_spmd(nc, in_maps, core_ids=list(range(8)))
    return res.results


def kernel(**inp):
    inp = {k: np.asarray(v) for k, v in inp.items()}
    r = _run(build_mods(), [_mods_inputs(inp, c) for c in range(8)])
    mod3 = np.zeros((2, 3, 12288), np.float32)
    for c in range(8):
        mo = r[c]["mo"]
        for l in range(2):
            mod3[l][:, c * 1536:(c + 1) * 1536] = mo[:, l].transpose(2, 1, 0).reshape(3, 1536)
    xT = [np.ascontiguousarray(np.concatenate([inp["ctx"][b], inp["x"][b]], 0).T) for b in range(2)]
    cst = rg_consts()
    r = _run(build_rg(), [rg_inputs(xT[c // 4], inp, mod3[0], inp["norm_g"][0], c // 4, c % 4, cst) for c in range(8)])
    m0 = [np.zeros((2048, 4352), BF) for _ in range(2)]
    for c in range(8):
        b, g = c // 4, c % 4
        mo = r[c]["moT"]
        m0[b][256 * g:256 * g + 256] = mo[0:256]
        m0[b][1024 + 256 * g:1024 + 256 * g + 256] = mo[256:512]
    def cols0(q):
        return np.concatenate([np.arange(64 * q, 64 * q + 64), 256 + np.arange(1024 * q, 1024 * q + 1024)])
    wo = tile_w(inp["ab_w_out"][0], 512)
    wi = tile_wi(inp["ffn_w_in"][0])
    w2 = tile_w(inp["ffn_w_out"][0], 128)
    ims = []
    for c in range(8):
        b, q = c // 4, c % 4
        cc = cols0(q)
        ims.append({"mT": np.ascontiguousarray(m0[b][:, cc]), "xT": np.ascontiguousarray(xT[b][:, cc]), "wo": wo, "wi": wi,
                    "w2": w2, "pv": post_pv(mod3[0], inp["norm_g"][0], b)})
    r = _run(build_post(16, [(64, 1), (512, 0), (512, 0)], 512), ims)
    h1T = [np.zeros((2048, 4352), np.float32) for _ in range(2)]
    for c in range(8):
        b, q = c // 4, c % 4
        h1T[b][:, cols0(q)] = r[c]["oT"]
    cs = rot_tables()
    r = _run(build_ret(), [ret_inputs(h1T[c // 4], inp["ret_w_in"][0], mod3[1], inp["norm_g"][1], c // 4, c % 4, cs) for c in range(8)])
    m1 = [np.zeros((4096, 4096), BF) for _ in range(2)]
    for c in range(8):
        b, hp = c // 4, c % 4
        m1[b][1024 * hp:1024 * hp + 1024] = r[c]["moT"]
    wo = tile_w(inp["ret_w_out"][0], 128)
    wi = tile_wi(inp["ffn_w_in"][1])
    w2 = tile_w(inp["ffn_w_out"][1], 128)
    ims = []
    for c in range(8):
        b, q = c // 4, c % 4
        ims.append({"mT": np.ascontiguousarray(m1[b][:, 1024 * q:1024 * q + 1024]),
                    "xT": np.ascontiguousarray(h1T[b][:, 256 + 1024 * q:256 + 1024 * q + 1024]), "wo": wo, "wi": wi,
                    "w2": w2, "pv": post_pv(mod3[1], inp["norm_g"][1], b)})
    r = _run(build_post(32, [(512, 0), (512, 0)], 128), ims)
    out = np.zeros((2, 4096, 2048), np.float32)
    for c in range(8):
        b, q = c // 4, c % 4
        out[b, 1024 * q:1024 * q + 1024, :] = r[c]["oT"].T
    return out
```
